# Optimizing a Trainium2 kernel written in Bass

```python
import math
import jax, jax.numpy as jnp
from jax import lax
import numpy as np

D_MODEL = 1024
BATCH = 8
SEQ = 4096
DEPTH = 2

PLE_DIM = 256
D_POOL = 512
POOL_WINDOWS = (2, 4, 8, 16)
N_POOL_GROUPS = len(POOL_WINDOWS)
POOL_GROUP = D_POOL // N_POOL_GROUPS
D_ATTN = D_MODEL - D_POOL
N_HEADS = 8
HEAD_DIM = D_ATTN // N_HEADS
MOBA_BLOCK = 256
MOBA_TOPK = 3
Q_CHUNK = 32
NUM_BUCKETS = 32
MAX_DISTANCE = 128
D_FF = 2816
CONV_WIDTH = 3
EPS = 1e-6
D_IN = D_POOL + 3 * D_ATTN

kernel_name = "hybrid_pool_moba_convffn_block"


def rms_norm(x, g):
    xf = x.astype(jnp.float32)
    y = xf * lax.rsqrt(jnp.mean(xf * xf, axis=-1, keepdims=True) + EPS)
    return (y * g.astype(jnp.float32)).astype(x.dtype)


def pool_mixer(u, w_pool, pool_scale):
    B, S, _ = u.shape
    uf = u.astype(jnp.float32)
    cpad = jnp.pad(jnp.cumsum(uf, axis=1), ((0, 0), (1, 0), (0, 0)))
    pos = jnp.arange(S)
    outs = []
    for g, w in enumerate(POOL_WINDOWS):
        c_g = cpad[:, :, g * POOL_GROUP:(g + 1) * POOL_GROUP]
        lagged = jnp.pad(c_g[:, :S + 1 - w], ((0, 0), (w - 1, 0), (0, 0)))
        count = jnp.minimum(pos + 1, w).astype(jnp.float32)[None, :, None]
        mean = (c_g[:, 1:] - lagged) / count
        outs.append(mean - uf[:, :, g * POOL_GROUP:(g + 1) * POOL_GROUP])
    pooled = jnp.stack(outs, axis=2)
    mixed = jnp.einsum('bsgc,gcd->bsgd', pooled, w_pool.astype(jnp.float32))
    return (mixed.reshape(B, S, D_POOL) * pool_scale.astype(jnp.float32)).astype(u.dtype)


def rel_bucket(dist):
    n = jnp.maximum(dist, 0)
    max_exact = NUM_BUCKETS // 2
    nf = jnp.maximum(n, 1).astype(jnp.float32)
    large = max_exact + (jnp.log(nf / max_exact) / math.log(MAX_DISTANCE / max_exact)
                         * (NUM_BUCKETS - max_exact)).astype(jnp.int32)
    large = jnp.minimum(large, NUM_BUCKETS - 1)
    return jnp.where(n < max_exact, n, large)


def moba_attention(q, k, v, rel_bias):
    B, S = q.shape[0], q.shape[1]
    nb = -(-S // MOBA_BLOCK)
    s_pad = nb * MOBA_BLOCK
    pad = ((0, 0), (0, s_pad - S), (0, 0), (0, 0))
    q, k, v = [jnp.pad(t, pad).transpose(0, 2, 1, 3) for t in (q, k, v)]
    k_blk = k.reshape(B, N_HEADS, nb, MOBA_BLOCK, HEAD_DIM)
    v_blk = v.reshape(B, N_HEADS, nb, MOBA_BLOCK, HEAD_DIM)
    k_mean = jnp.mean(k_blk.astype(jnp.float32), axis=3)
    gate = jnp.einsum('bhsd,bhnd->bhsn', q.astype(jnp.float32), k_mean)
    q_blk_idx = jnp.arange(s_pad) // MOBA_BLOCK
    past = jnp.arange(nb)[None, :] < q_blk_idx[:, None]
    gate = jnp.where(past, gate, -jnp.inf)
    n_sel = min(MOBA_TOPK, nb)
    _, sel = lax.top_k(gate, n_sel)
    sel_valid = sel < q_blk_idx[:, None]

    scale = HEAD_DIM ** -0.5
    bias_t = rel_bias.astype(jnp.float32).T
    b_idx = jnp.arange(B)[:, None, None, None]
    h_idx = jnp.arange(N_HEADS)[None, :, None, None]
    key_off = jnp.arange(MOBA_BLOCK)
    n_sel_keys = n_sel * MOBA_BLOCK

    def chunk(ci):
        start = ci * Q_CHUNK
        qc = lax.dynamic_slice_in_dim(q, start, Q_CHUNK, axis=2)
        sc = lax.dynamic_slice_in_dim(sel, start, Q_CHUNK, axis=2)
        vc = lax.dynamic_slice_in_dim(sel_valid, start, Q_CHUNK, axis=2)
        t = start + jnp.arange(Q_CHUNK)
        j = start // MOBA_BLOCK
        k_sel = k_blk[b_idx, h_idx, sc]
        v_sel = v_blk[b_idx, h_idx, sc]
        key_pos = sc[..., None] * MOBA_BLOCK + key_off
        bias_sel = bias_t[h_idx[..., None], rel_bucket(t[:, None, None] - key_pos)]
        l_sel = jnp.einsum('bhqd,bhqnkd->bhqnk', qc, k_sel).astype(jnp.float32) * scale + bias_sel
        l_sel = jnp.where(vc[..., None], l_sel, -jnp.inf)
        k_own = lax.dynamic_index_in_dim(k_blk, j, axis=2, keepdims=False)
        v_own = lax.dynamic_index_in_dim(v_blk, j, axis=2, keepdims=False)
        d_own = t[:, None] - (j * MOBA_BLOCK + key_off)[None, :]
        bias_own = bias_t[:, rel_bucket(d_own)]
        l_own = jnp.einsum('bhqd,bhkd->bhqk', qc, k_own).astype(jnp.float32) * scale + bias_own
        l_own = jnp.where(d_own >= 0, l_own, -jnp.inf)
        logits = jnp.concatenate([l_sel.reshape(B, N_HEADS, Q_CHUNK, n_sel_keys), l_own], axis=-1)
        probs = jax.nn.softmax(logits, axis=-1).astype(v.dtype)
        p_sel = probs[..., :n_sel_keys].reshape(B, N_HEADS, Q_CHUNK, n_sel, MOBA_BLOCK)
        p_own = probs[..., n_sel_keys:]
        return (jnp.einsum('bhqnk,bhqnkd->bhqd', p_sel, v_sel)
                + jnp.einsum('bhqk,bhkd->bhqd', p_own, v_own))

    outs = lax.map(chunk, jnp.arange(s_pad // Q_CHUNK))
    out = outs.transpose(1, 0, 3, 2, 4).reshape(B, s_pad, N_HEADS, HEAD_DIM)[:, :S]
    return out.reshape(B, S, D_ATTN)


def conv_ffn(h, w_up, conv_w, conv_b, w_down):
    S = h.shape[1]
    gate, val = jnp.split(h @ w_up, 2, axis=-1)
    gp = jnp.pad(gate, ((0, 0), (CONV_WIDTH - 1, 0), (0, 0)))
    conv = conv_b + gp[:, 0:S] * conv_w[0]
    for kk in range(1, CONV_WIDTH):
        conv = conv + gp[:, kk:kk + S] * conv_w[kk]
    return (jax.nn.gelu(conv, approximate=True) * val) @ w_down


def setup_inputs(seed: int = 0) -> dict:
    key = jax.random.key(seed)
    ks = jax.random.split(key, 20)
    f32 = jnp.float32
    nrm = lambda k, shape, s: jax.random.normal(k, shape, f32) * s
    gain = lambda k: 1.0 + nrm(k, (DEPTH, D_MODEL), 0.05)
    return {
        "x": nrm(ks[0], (BATCH, SEQ, D_MODEL), 1.0),
        "p": nrm(ks[1], (DEPTH, BATCH, SEQ, PLE_DIM), 1.0),
        "rel_bias": nrm(ks[2], (NUM_BUCKETS, N_HEADS), 0.5),
        "g_mix_pre": gain(ks[3]),
        "g_mix_post": gain(ks[4]),
        "g_ffn_pre": gain(ks[5]),
        "g_ffn_post": gain(ks[6]),
        "w_in": nrm(ks[7], (DEPTH, D_MODEL, D_IN), D_MODEL ** -0.5),
        "w_pool": nrm(ks[8], (DEPTH, N_POOL_GROUPS, POOL_GROUP, POOL_GROUP), POOL_GROUP ** -0.5),
        "pool_scale": 1.0 + nrm(ks[9], (DEPTH, D_POOL), 0.1),
        "w_out": nrm(ks[10], (DEPTH, D_MODEL, D_MODEL), D_MODEL ** -0.5),
        "w_up": nrm(ks[11], (DEPTH, D_MODEL, 2 * D_FF), D_MODEL ** -0.5),
        "conv_w": nrm(ks[12], (DEPTH, CONV_WIDTH, D_FF), CONV_WIDTH ** -0.5),
        "conv_b": nrm(ks[13], (DEPTH, D_FF), 0.02),
        "w_down": nrm(ks[14], (DEPTH, D_FF, D_MODEL), D_FF ** -0.5),
        "w_ple": nrm(ks[15], (DEPTH, PLE_DIM, D_MODEL), PLE_DIM ** -0.5),
        "w_ple_gate": nrm(ks[16], (DEPTH, D_MODEL, D_MODEL), D_MODEL ** -0.5),
    }


def reference(x, p, rel_bias, g_mix_pre, g_mix_post, g_ffn_pre, g_ffn_post, w_in, w_pool,
              pool_scale, w_out, w_up, conv_w, conv_b, w_down, w_ple, w_ple_gate):
    B, S, _ = x.shape
    for i in range(DEPTH):
        h = rms_norm(x, g_mix_pre[i])
        proj = h @ w_in[i]
        u, q, k, v = jnp.split(proj, [D_POOL, D_POOL + D_ATTN, D_POOL + 2 * D_ATTN], axis=-1)
        y_pool = pool_mixer(u, w_pool[i], pool_scale[i])
        y_attn = moba_attention(q.reshape(B, S, N_HEADS, HEAD_DIM),
                                k.reshape(B, S, N_HEADS, HEAD_DIM),
                                v.reshape(B, S, N_HEADS, HEAD_DIM), rel_bias)
        y = jnp.concatenate([y_pool, y_attn], axis=-1) @ w_out[i]
        x = x + rms_norm(y, g_mix_post[i])
        h = rms_norm(x, g_ffn_pre[i])
        x = x + rms_norm(conv_ffn(h, w_up[i], conv_w[i], conv_b[i], w_down[i]), g_ffn_post[i])
        x = x + (p[i] @ w_ple[i]) * jax.nn.sigmoid(x @ w_ple_gate[i])
    return x
```

```python
import numpy as np
import concourse.bass as bass
import concourse.mybir as mybir

F32 = mybir.dt.float32
BF16 = mybir.dt.bfloat16
AF = mybir.ActivationFunctionType
ALU = mybir.AluOpType
AX = mybir.AxisListType

EPOCH = 30000
NDMA_SEMS = 8


class Tok:
    __slots__ = ("src", "sem", "val")

    def __init__(self, src, sem, val):
        self.src = src
        self.sem = sem
        self.val = val


class Buf:
    __slots__ = ("name", "w", "r")

    def __init__(self, name=""):
        self.name = name
        self.w = None
        self.r = []


class Queue:
    def __init__(self, S, name, obj, is_pe=False):
        self.S = S
        self.name = name
        self.obj = obj
        self.is_pe = is_pe
        self.seen = {}
        self.sems = []
        self.count = 0
        self.pending = None
        self.last = None
        self.dma_sems = []
        self.dma_k = 0
        self.dma_toks = []

    def _cur_sem(self):
        if not self.sems or self.count >= EPOCH:
            self.sems.append(self.S.new_sem(f"{self.name}_e{len(self.sems)}"))
            self.count = 0
        return self.sems[-1]

    def wait(self, tok):
        key = id(tok.sem)
        if self.seen.get(key, 0) >= tok.val:
            return
        self.obj.wait_ge(tok.sem, tok.val)
        self.seen[key] = tok.val
        self.S.nwaits += 1


class Sched:
    def __init__(self, nc):
        self.nc = nc
        self._sem_ctx = []
        self.nwaits = 0
        self.ninstr = 0
        self.pe = Queue(self, "pe", nc.tensor, is_pe=True)
        self.act = Queue(self, "act", nc.scalar)
        self.dve = Queue(self, "dve", nc.vector)
        self.pool = Queue(self, "pool", nc.gpsimd)
        self.sp = Queue(self, "sp", nc.sync)
        self.queues = [self.pe, self.act, self.dve, self.pool, self.sp]

    def new_sem(self, name):
        ctx = self.nc.semaphore(name)
        sem = ctx.__enter__()
        self._sem_ctx.append(ctx)
        return sem

    def close(self):
        for ctx in reversed(self._sem_ctx):
            ctx.__exit__(None, None, None)
        self._sem_ctx = []

    def op(self, q, fn, reads=(), writes=(), excl=(), inc=True):
        waits = []
        for b in reads:
            if b.w is not None and not (q.is_pe and b.w.src is q):
                waits.append(b.w)
        for b in writes:
            if b.w is not None and b.w.src is not q:
                waits.append(b.w)
            for t in b.r:
                if t.src is not q:
                    waits.append(t)
        for b in excl:
            if b.w is not None and b.w.src is not q:
                waits.append(b.w)
            for t in b.r:
                if t.src is not q:
                    waits.append(t)
        for t in waits:
            q.wait(t)
        ins = fn()
        self.ninstr += 1
        if q.pending is None:
            sem = q._cur_sem()
            q.pending = Tok(q, sem, q.count + 1)
        tok = q.pending
        if inc:
            ins.then_inc(tok.sem, 1)
            q.count += 1
            q.pending = None
            q.last = tok
        for b in reads:
            b.r = [t for t in b.r if t.src is not q]
            b.r.append(tok)
        for b in writes:
            b.w = tok
            b.r = []
        for b in excl:
            b.w = tok
            b.r = []
        return tok

    def dma(self, q, out, in_, reads=(), writes=(), **kw):
        waits = []
        for b in reads:
            if b.w is not None:
                waits.append(b.w)
        for b in writes:
            if b.w is not None:
                waits.append(b.w)
            waits.extend(b.r)
        for t in waits:
            q.wait(t)
        k = q.dma_k
        if len(q.dma_sems) < NDMA_SEMS:
            q.dma_sems.append(self.new_sem(f"{q.name}_dma{len(q.dma_sems)}"))
            q.dma_toks.append(None)
        slot = k % NDMA_SEMS
        prev = q.dma_toks[slot]
        if prev is not None:
            q.wait(prev)
        sem = q.dma_sems[slot]
        val = 16 * (k // NDMA_SEMS + 1)
        ins = q.obj.dma_start(out=out, in_=in_, **kw)
        ins.then_inc(sem, 16)
        self.ninstr += 1
        tok = Tok(("dma", q.name), sem, val)
        q.dma_toks[slot] = tok
        q.dma_k += 1
        for b in reads:
            b.r.append(tok)
        for b in writes:
            b.w = tok
            b.r = []
        return tok

    def barrier(self):
        toks = []
        for q in self.queues:
            assert q.pending is None, f"pending non-inc'd instruction on {q.name}"
            if q.last is not None:
                toks.append(q.last)
            for t in q.dma_toks:
                if t is not None:
                    toks.append(t)
        for q in self.queues:
            for t in toks:
                if t.src is q:
                    continue
                q.wait(t)

    def finish(self, q=None):
        q = q or self.sp
        toks = []
        for qq in self.queues:
            if qq.last is not None and qq is not q:
                toks.append(qq.last)
            for t in qq.dma_toks:
                if t is not None:
                    toks.append(t)
        for t in toks:
            q.wait(t)
import contextlib

SEQ = 4096
DM = 1024
T = 512
NT = SEQ // T
DFF = 2816
NFC = DFF // 128
BIG = 30000.0
EPS = 1e-6


def build_program(nc, debug=False, stop_after=None, layers=(0, 1)):
    S = Sched(nc)
    kind_s = "ExternalOutput" if debug else "Internal"

    def din(name, shape, dt=F32):
        return nc.dram_tensor(name, list(shape), dt, kind="ExternalInput").ap()

    xT = din("xT", [DM, SEQ])
    pT = din("pT", [2, 256, SEQ])
    w_in = din("w_in", [2, DM, 2048])
    w_pool = din("w_pool", [2, 4, 128, 128])
    w_out = din("w_out", [2, DM, DM])
    w_up = din("w_up", [2, DM, 2 * DFF])
    w_down = din("w_down", [2, DFF, DM])
    w_ple = din("w_ple", [2, 256, DM])
    w_gate = din("w_gate", [2, DM, DM])
    gvec = din("gvec", [128, 2, 4, 8])
    pscale = din("pscale", [128, 2, 4])
    convw = din("convw", [128, 2, NFC, 3])
    convb = din("convb", [128, 2, NFC])
    ident_d = din("ident", [128, 128], BF16)
    onehot_d = din("onehot", [16, SEQ], BF16)
    pmall_d = din("pmall", [128, 32, 16])
    bigpast_d = din("bigpast", [128, 32, 16])
    tb_d = din("tb", [8, 128, 2, 2, 256])
    cm_d = din("cm", [128, 2, 256])
    b31_d = din("b31", [128, 8])
    rc_d = din("rc", [128, 4, 16])

    outT = nc.dram_tensor("outT", [DM, SEQ], F32, kind="ExternalOutput").ap()
    xs = nc.dram_tensor("xs", [DM, SEQ], F32, kind=kind_s).ap()
    QT = nc.dram_tensor("QT", [512, SEQ], BF16, kind=kind_s).ap()
    KT = nc.dram_tensor("KT", [512, SEQ], BF16, kind=kind_s).ap()
    VS = nc.dram_tensor("VS", [SEQ, 520], BF16, kind=kind_s).ap()
    CAT = nc.dram_tensor("CAT", [DM, SEQ], BF16, kind=kind_s).ap()

    b_xs = [[Buf(f"xs{i}_{c}") for c in range(8)] for i in range(NT)]
    b_qt = [Buf() for _ in range(NT)]
    b_kt = [Buf() for _ in range(NT)]
    b_vs = [Buf() for _ in range(NT)]
    b_catp = [Buf() for _ in range(NT)]
    b_cath = [Buf() for _ in range(8)]
    b_out = [Buf() for _ in range(NT)]

    pe, act, dve, pool, sp = S.pe, S.act, S.dve, S.pool, S.sp
    V, A, P, PE = nc.vector, nc.scalar, nc.gpsimd, nc.tensor

    es0 = contextlib.ExitStack()

    uid = [0]

    def sbt(es, name, shape, dt):
        uid[0] += 1
        return es.enter_context(nc.sbuf_tensor(f"sb{uid[0]}_{name}", list(shape), dt))

    banks = [es0.enter_context(nc.psum_tensor(f"bank{i}", [128, 512], F32)) for i in range(8)]
    b_bank = [Buf(f"bank{i}") for i in range(8)]
    rr = [0]

    def next_bank(lo=0, hi=8):
        i = lo + rr[0] % (hi - lo)
        rr[0] += 1
        return banks[i], b_bank[i]

    ident = sbt(es0, "ident", [128, 128], BF16); b_ident = Buf()
    ones_b = sbt(es0, "ones_b", [128, 128], BF16); b_ones = Buf()
    ones32 = sbt(es0, "ones32", [128, 64], F32); b_ones32 = Buf()
    gv = sbt(es0, "gv", [128, 2, 4, 8], F32); b_gv = Buf()
    psc = sbt(es0, "psc", [128, 2, 4], F32); b_psc = Buf()
    cw = sbt(es0, "cw", [128, 2, NFC, 3], F32); b_cw = Buf()
    cb = sbt(es0, "cb", [128, 2, NFC], F32); b_cb = Buf()
    S.dma(sp, ident[:], ident_d, writes=[b_ident])
    S.dma(sp, gv[:], gvec, writes=[b_gv])
    S.dma(sp, psc[:], pscale, writes=[b_psc])
    S.dma(sp, cw[:], convw, writes=[b_cw])
    S.dma(sp, cb[:], convb, writes=[b_cb])
    S.op(dve, lambda: V.memset(ones_b[:], 1.0), writes=[b_ones])
    S.op(dve, lambda: V.memset(ones32[:], 1.0), writes=[b_ones32])

    def cols(it):
        return slice(it * T, (it + 1) * T)

    def fm(ap):
        return ap.rearrange("(c p) t -> p c t", p=128)

    def load_weight(dst, src_view, nchunk, bufs):
        for kc in range(nchunk):
            S.dma(pool, dst[:, kc, :], src_view[:, kc, :], writes=[bufs[kc]])

    def rms_rstd(src, b_src, sq, b_sq, R, b_R):
        for c in range(8):
            S.op(act, lambda c=c: A.activation(sq[:, c, :], src[:, c, :], AF.Square),
                 reads=[b_src[c]], writes=[b_sq[c]])
        bk, bb = next_bank()
        for c in range(8):
            S.op(pe, lambda c=c: PE.matmul(bk[:, :], ones_b[:, :], sq[:, c, :], start=(c == 0), stop=(c == 7)),
                 reads=[b_sq[c], b_ones], excl=[bb], inc=(c == 7))
        S.op(act, lambda: A.activation(R[:, :], bk[:, :], AF.Sqrt, bias=eps_t[:, 0:1], scale=1.0 / DM),
             reads=[b_eps], excl=[bb], writes=[b_R])
        S.op(dve, lambda: V.reciprocal(R[:, :], R[:, :]), reads=[b_R], writes=[b_R])

    eps_t = sbt(es0, "eps_t", [128, 1], F32); b_eps = Buf()
    S.op(dve, lambda: V.memset(eps_t[:], EPS), writes=[b_eps])

    def pre_norm(xt, b_x, H, b_H, R, b_R, l, which):
        rms_rstd(xt, b_x, H, b_H, R, b_R)
        for c in range(8):
            q, E = dve, V
            S.op(q, lambda c=c, E=E: E.scalar_tensor_tensor(
                H[:, c, :], xt[:, c, :], gv[:, l, which, c:c + 1], R[:, :], ALU.mult, ALU.mult),
                reads=[b_x[c], b_gv, b_R], writes=[b_H[c]])

    def post_norm_residual(Y, b_Y, sq, b_sq, R, b_R, xt, b_x, l, which):
        rms_rstd(Y, b_Y, sq, b_sq, R, b_R)
        for c in range(8):
            S.op(dve, lambda c=c: V.scalar_tensor_tensor(
                Y[:, c, :], Y[:, c, :], gv[:, l, which, c:c + 1], R[:, :], ALU.mult, ALU.mult),
                reads=[b_Y[c], b_gv, b_R], writes=[b_Y[c]])
            S.op(pool, lambda c=c: P.tensor_tensor(xt[:, c, :], xt[:, c, :], Y[:, c, :], ALU.add),
                 reads=[b_x[c], b_Y[c]], writes=[b_x[c]])

    def ple_load_p(l, it, PB, b_PB):
        S.dma(pool, PB[:, :, :], fm(pT[l])[:, :, cols(it)], writes=[b_PB])

    def ple(l, it, xt, b_x, XB, b_XB, PB, b_PB, Wg, b_Wg, Wp2, b_Wp2, SG, b_SG, TM, b_TM):
        for c in range(8):
            S.op(act, lambda c=c: A.copy(XB[:, c, :], xt[:, c, :]), reads=[b_x[c]], writes=[b_XB[c]])
        for oc in range(8):
            oc_s = slice(oc * 128, (oc + 1) * 128)
            bg, bbg = next_bank()
            for kc in range(8):
                S.op(pe, lambda kc=kc: PE.matmul(bg[:, :], Wg[:, kc, oc_s], XB[:, kc, :], start=(kc == 0), stop=(kc == 7)),
                     reads=[b_Wg[kc], b_XB[kc]], excl=[bbg], inc=(kc == 7))
            sg, bsg = SG[oc % 2], b_SG[oc % 2]
            S.op(act, lambda: A.activation(sg[:, :], bg[:, :], AF.Sigmoid), excl=[bbg], writes=[bsg])
            be, bbe = next_bank()
            for kc in range(2):
                S.op(pe, lambda kc=kc: PE.matmul(be[:, :], Wp2[:, kc, oc_s], PB[:, kc, :], start=(kc == 0), stop=(kc == 1)),
                     reads=[b_Wp2[kc], b_PB], excl=[bbe], inc=(kc == 1))
            tm, btm = TM[oc % 2], b_TM[oc % 2]
            S.op(dve, lambda: V.tensor_tensor(tm[:, :], be[:, :], sg[:, :], ALU.mult),
                 reads=[bsg], excl=[bbe], writes=[btm])
            S.op(dve, lambda oc=oc: V.tensor_tensor(xt[:, oc, :], xt[:, oc, :], tm[:, :], ALU.add),
                 reads=[btm, b_x[oc]], writes=[b_x[oc]])

    def phase_A(l):
        with contextlib.ExitStack() as es:
            Win = sbt(es, "Win", [128, 8, 2048], BF16); b_Win = [Buf() for _ in range(8)]
            Wpl = sbt(es, "Wpl", [128, 4, 128], BF16); b_Wpl = [Buf()]
            load_weight(Win, w_in[l].rearrange("(kc p) n -> p kc n", p=128), 8, b_Win)
            S.dma(pool, Wpl[:, :, :], w_pool[l].rearrange("g c d -> c g d"), writes=b_Wpl)
            if l > 0:
                Wg = sbt(es, "Wg", [128, 8, DM], BF16); b_Wg = [Buf() for _ in range(8)]
                Wp2 = sbt(es, "Wp2", [128, 2, DM], BF16); b_Wp2 = [Buf() for _ in range(2)]
                load_weight(Wg, w_gate[l - 1].rearrange("(kc p) n -> p kc n", p=128), 8, b_Wg)
                load_weight(Wp2, w_ple[l - 1].rearrange("(kc p) n -> p kc n", p=128), 2, b_Wp2)
                XBs = [sbt(es, f"XB{i}", [128, 8, T], BF16) for i in range(2)]
                b_XBs = [[Buf() for _ in range(8)] for _ in range(2)]
                PBs = [sbt(es, f"PB{i}", [128, 2, T], BF16) for i in range(2)]; b_PBs = [Buf(), Buf()]
                SG = [sbt(es, f"SG{i}", [128, T], F32) for i in range(2)]; b_SG = [Buf(), Buf()]
                TM = [sbt(es, f"TM{i}", [128, T], F32) for i in range(2)]; b_TM = [Buf(), Buf()]
            NXT = 3
            XT = [sbt(es, f"XT{i}", [128, 8, T], F32) for i in range(NXT)]
            b_XT = [[Buf() for _ in range(8)] for _ in range(NXT)]
            Hs = [sbt(es, f"H{i}", [128, 8, T], BF16) for i in range(2)]
            b_Hs = [[Buf() for _ in range(8)] for _ in range(2)]
            R = sbt(es, "R", [128, T], F32); b_R = Buf()
            U = sbt(es, "U", [128, 4, 16 + T], F32); b_U = [Buf() for _ in range(4)]
            TA = sbt(es, "TA", [128, 16 + T], F32); b_TA = Buf()
            TB_ = sbt(es, "TB_", [128, 16 + T], F32); b_TB = Buf()
            T16 = sbt(es, "T16", [128, 16], F32); b_T16 = Buf()
            RC = sbt(es, "RC", [128, 4, 16], F32); b_RC = Buf()
            PLs = [sbt(es, f"PL{i}", [128, 4, T], BF16) for i in range(2)]
            b_PLs = [[Buf() for _ in range(4)] for _ in range(2)]
            QS = [sbt(es, f"QS{i}", [128, 4, T], BF16) for i in range(2)]; b_QS = [Buf(), Buf()]
            KS = [sbt(es, f"KS{i}", [128, 4, T], BF16) for i in range(2)]; b_KS = [Buf(), Buf()]
            VSt = [sbt(es, f"VSt{i}", [128, 4, 8, 65], BF16) for i in range(2)]; b_VSt = [Buf(), Buf()]
            for i in range(2):
                S.op(pool, lambda i=i: P.memset(VSt[i][:, :, :, 64:65], 1.0), writes=[b_VSt[i]])
            CP = [sbt(es, f"CP{i}", [128, 4, T], BF16) for i in range(2)]; b_CP = [Buf(), Buf()]
            S.dma(sp, RC[:], rc_d, writes=[b_RC])
            S.op(pool, lambda: P.memset(U[:, :, 0:16], 0.0), writes=b_U)
            src = xT if l == 0 else xs
            W = 16 + T

            def load_x(it):
                xt, bx = XT[it % NXT], b_XT[it % NXT]
                rd = [] if l == 0 else b_xs[it]
                S.dma(sp, xt[:, :, :], fm(src)[:, :, cols(it)], reads=rd, writes=bx)

            def stage1a(it):
                xt, bx = XT[it % NXT], b_XT[it % NXT]
                H, b_H = Hs[it % 2], b_Hs[it % 2]
                if l > 0:
                    if it + 1 < NT:
                        ple_load_p(l - 1, it + 1, PBs[(it + 1) % 2], b_PBs[(it + 1) % 2])
                    ple(l - 1, it, xt, bx, XBs[it % 2], b_XBs[it % 2], PBs[it % 2], b_PBs[it % 2],
                        Wg, b_Wg, Wp2, b_Wp2, SG, b_SG, TM, b_TM)
                    S.dma(sp, fm(xs)[:, :, cols(it)], xt[:, :, :], reads=bx, writes=b_xs[it])
                for c in range(8):
                    S.op(act, lambda c=c: A.activation(H[:, c, :], xt[:, c, :], AF.Square), reads=[bx[c]], writes=[b_H[c]])

            def stage1b(it):
                xt, bx = XT[it % NXT], b_XT[it % NXT]
                H, b_H = Hs[it % 2], b_Hs[it % 2]
                bk, bb = next_bank()
                for c in range(8):
                    S.op(pe, lambda c=c: PE.matmul(bk[:, :], ones_b[:, :], H[:, c, :], start=(c == 0), stop=(c == 7)),
                         reads=[b_H[c], b_ones], excl=[bb], inc=(c == 7))
                S.op(act, lambda: A.activation(R[:, :], bk[:, :], AF.Sqrt, bias=eps_t[:, 0:1], scale=1.0 / DM),
                     reads=[b_eps], excl=[bb], writes=[b_R])
                S.op(dve, lambda: V.reciprocal(R[:, :], R[:, :]), reads=[b_R], writes=[b_R])
                for c in range(8):
                    S.op(dve, lambda c=c: V.scalar_tensor_tensor(H[:, c, :], xt[:, c, :], gv[:, l, 0, c:c + 1], R[:, :], ALU.mult, ALU.mult),
                         reads=[bx[c], b_gv, b_R], writes=[b_H[c]])

            def proj_group(it, oc):
                H, b_H = Hs[it % 2], b_Hs[it % 2]
                par = it % 2
                oc_s = slice(oc * 128, (oc + 1) * 128)
                bk, bb = next_bank()
                for kc in range(8):
                    S.op(pe, lambda kc=kc: PE.matmul(bk[:, :], Win[:, kc, oc_s], H[:, kc, :], start=(kc == 0), stop=(kc == 7)),
                         reads=[b_Win[kc], b_H[kc]], excl=[bb], inc=(kc == 7))
                if oc < 4:
                    S.op(act, lambda: A.copy(U[:, oc, 16:16 + T], bk[:, :]), excl=[bb], writes=[b_U[oc]])
                elif oc < 8:
                    S.op(act, lambda: A.mul(QS[par][:, oc - 4, :], bk[:, :], 0.125), excl=[bb], writes=[b_QS[par]])
                    if oc == 7:
                        S.dma(sp, fm(QT)[:, :, cols(it)], QS[par][:, :, :], reads=[b_QS[par]], writes=[b_qt[it]])
                else:
                    S.op(dve, lambda: V.tensor_copy(KS[par][:, oc - 8, :], bk[:, :]), excl=[bb], writes=[b_KS[par]])
                    if oc == 11:
                        S.dma(sp, fm(KT)[:, :, cols(it)], KS[par][:, :, :], reads=[b_KS[par]], writes=[b_kt[it]])

            def proj_v(it):
                H, b_H = Hs[it % 2], b_Hs[it % 2]
                par = it % 2
                for sub in range(4):
                    bk, bb = next_bank()
                    for kc in range(8):
                        S.op(pe, lambda kc=kc: PE.matmul(bk[:, :], H[:, kc, sub * 128:(sub + 1) * 128], Win[:, kc, 1536:2048],
                                                         start=(kc == 0), stop=(kc == 7)),
                             reads=[b_Win[kc], b_H[kc]], excl=[bb], inc=(kc == 7))
                    bkv = bk[:, :].rearrange("p (h d) -> p h d", h=8)
                    if sub % 2 == 0:
                        S.op(dve, lambda: V.tensor_copy(VSt[par][:, sub, :, 0:64], bkv), excl=[bb], writes=[b_VSt[par]])
                    else:
                        S.op(act, lambda: A.copy(VSt[par][:, sub, :, 0:64], bkv), excl=[bb], writes=[b_VSt[par]])
                S.dma(sp, VS.rearrange("(s p) f -> p s f", p=128)[:, it * 4:(it + 1) * 4, :],
                      VSt[par][:, :, :, :].rearrange("p s h d -> p s (h d)"),
                      reads=[b_VSt[par]], writes=[b_vs[it]])

            def poolE(it):
                PL, b_PL = PLs[it % 2], b_PLs[it % 2]
                for g in range(4):
                    Ug = U[:, g, :]
                    S.op(pool, lambda: P.tensor_tensor(TA[:, 1:W], Ug[:, 1:W], Ug[:, 0:W - 1], ALU.add),
                         reads=[b_U[g]], writes=[b_TA])
                    fin, bfin = TA, b_TA
                    if g >= 1:
                        S.op(pool, lambda: P.tensor_tensor(TB_[:, 3:W], TA[:, 3:W], TA[:, 1:W - 2], ALU.add),
                             reads=[b_TA], writes=[b_TB])
                        fin, bfin = TB_, b_TB
                    if g >= 2:
                        S.op(pool, lambda: P.tensor_tensor(TA[:, 7:W], TB_[:, 7:W], TB_[:, 3:W - 4], ALU.add),
                             reads=[b_TB], writes=[b_TA])
                        fin, bfin = TA, b_TA
                    if g >= 3:
                        S.op(pool, lambda: P.tensor_tensor(TB_[:, 15:W], TA[:, 15:W], TA[:, 7:W - 8], ALU.add),
                             reads=[b_TA], writes=[b_TB])
                        fin, bfin = TB_, b_TB
                    wsz = float(2 ** (g + 1))
                    S.op(dve, lambda: V.scalar_tensor_tensor(PL[:, g, :], fin[:, 16:W], 1.0 / wsz, Ug[:, 16:W], ALU.mult, ALU.subtract),
                         reads=[bfin, b_U[g]], writes=[b_PL[g]])
                    if it == 0:
                        S.op(pool, lambda: P.tensor_tensor(T16[:, :], fin[:, 16:32], RC[:, g, :], ALU.mult),
                             reads=[bfin, b_RC], writes=[b_T16])
                        S.op(pool, lambda: P.tensor_tensor(PL[:, g, 0:16], T16[:, :], Ug[:, 16:32], ALU.subtract),
                             reads=[b_T16, b_U[g]], writes=[b_PL[g]])
                    S.op(pool, lambda: P.tensor_copy(U[:, g, 0:16], U[:, g, T:T + 16]), reads=[b_U[g]], writes=[b_U[g]])

            def poolM(it):
                PL, b_PL = PLs[it % 2], b_PLs[it % 2]
                par = it % 2
                for g in range(4):
                    bk, bb = next_bank()
                    S.op(pe, lambda: PE.matmul(bk[:, :], Wpl[:, g, :], PL[:, g, :], start=True, stop=True),
                         reads=[b_Wpl[0], b_PL[g]], excl=[bb])
                    S.op(act, lambda: A.mul(CP[par][:, g, :], bk[:, :], psc[:, l, g:g + 1]),
                         reads=[b_psc], excl=[bb], writes=[b_CP[par]])
                S.dma(sp, fm(CAT)[:, 0:4, cols(it)], CP[par][:, :, :], reads=[b_CP[par]], writes=[b_catp[it]])

            load_x(0)
            load_x(1)
            if l > 0:
                ple_load_p(l - 1, 0, PBs[0], b_PBs[0])
            stage1a(0)
            stage1b(0)
            for it in range(NT):
                if it + 2 < NT:
                    load_x(it + 2)
                if it + 1 < NT:
                    stage1a(it + 1)
                for oc in range(0, 6):
                    proj_group(it, oc)
                if it + 1 < NT:
                    stage1b(it + 1)
                for oc in range(6, 12):
                    proj_group(it, oc)
                proj_v(it)
                if it > 0:
                    poolM(it - 1)
                poolE(it)
            poolM(NT - 1)
        S.barrier()

    def phase_B(l, jobs=()):
        with contextlib.ExitStack() as es:
            jobs = list(jobs)
            WST = [sbt(es, f"WST{i}", [128, DFF // 2], F32) for i in range(2)]; b_WST = [Buf(), Buf()]
            jstate = {"dma": 0, "cast": 0, "g": 0}

            def job_tick():
                g = jstate["g"]
                jstate["g"] += 1
                if not jobs:
                    return
                k = jstate["dma"]
                if k < len(jobs) and g == 20 + 40 * k:
                    n = jobs[k][1].shape[-1]
                    S.dma(sp, WST[k % 2][:, 0:n], jobs[k][1], writes=[b_WST[k % 2]])
                    jstate["dma"] += 1
                k = jstate["cast"]
                if k < len(jobs) and g == 20 + 40 * k + 24:
                    dst, src_, bdst = jobs[k]
                    n = src_.shape[-1]
                    S.op(dve, lambda: V.tensor_copy(dst, WST[k % 2][:, 0:n]), reads=[b_WST[k % 2]], writes=[bdst])
                    jstate["cast"] += 1

            QA = [sbt(es, f"QA{i}", [128, SEQ], BF16) for i in range(2)]
            KA = [sbt(es, f"KA{i}", [128, SEQ], BF16) for i in range(2)]
            b_QA = [Buf(), Buf()]; b_QAm = [Buf(), Buf()]; b_KA = [Buf(), Buf()]; b_KAo = [Buf(), Buf()]
            VA = sbt(es, "VA", [128, 32, 8, 65], BF16); b_VA = [Buf() for _ in range(4)]; b_VA1 = Buf()
            OT1 = sbt(es, "OT", [64, SEQ], BF16); OT = [OT1, OT1]; b_OT1 = Buf(); b_OT = [b_OT1, b_OT1]
            PMs = sbt(es, "PMs", [128, 32, 16], F32); b_PM = Buf()
            BPs = sbt(es, "BPs", [128, 32, 16], F32); b_BP = Buf()
            CMs = sbt(es, "CMs", [128, 2, 256], F32); b_CM = Buf()
            B31 = sbt(es, "B31", [128, 8], F32); b_B31 = Buf()
            TBs1 = sbt(es, "TBs", [128, 2, 2, 256], F32); TBs = [TBs1, TBs1]; b_TBs1 = Buf(); b_TBs = [b_TBs1, b_TBs1]
            BT = [sbt(es, f"BT{i}", [128, 2, 2, 256], BF16) for i in range(2)]; b_BT = [Buf(), Buf()]
            KM32 = sbt(es, "KM32", [64, 16], F32); b_KM32 = Buf()
            KMb = [sbt(es, f"KMb{i}", [64, 16], BF16) for i in range(2)]; b_KMb = [Buf(), Buf()]
            Gs = sbt(es, "Gs", [128, 32, 16], F32); b_Gs = Buf()
            T8 = sbt(es, "T8", [128, 32, 8], F32); b_T8 = Buf()
            Ts = sbt(es, "Ts", [128, 32, 16], F32); b_Ts = Buf()
            MBP = sbt(es, "MBP", [128, 32, 80], BF16); b_MBP = Buf()
            NPT = 4
            PT = [sbt(es, f"PT{i}", [128, 512], BF16) for i in range(NPT)]; b_PT = [Buf() for _ in range(NPT)]
            NOS = 4
            OS = [sbt(es, f"OS{i}", [64, 256], F32) for i in range(NOS)]; b_OS = [Buf() for _ in range(NOS)]
            RR = [sbt(es, f"RR{i}", [128, 256], F32) for i in range(NOS)]; b_RR = [Buf() for _ in range(NOS)]

            S.dma(sp, PMs[:], pmall_d, writes=[b_PM])
            S.dma(sp, BPs[:], bigpast_d, writes=[b_BP])
            S.dma(sp, CMs[:], cm_d, writes=[b_CM])
            S.dma(sp, B31[:], b31_d, writes=[b_B31])
            for i in range(2):
                S.dma(sp, KA[i][64:80, :], onehot_d, writes=[b_KAo[i]])
            S.op(pool, lambda: P.memset(MBP[:, :, :], 0.0), writes=[b_MBP])

            SB = [0, 1, 2, 3]
            OB = [4, 5]
            XB_ = [6, 7]

            def prologue_stages(h):
                hp = h % 2
                qa, ka = QA[hp], KA[hp]
                st = []

                def s_load():
                    S.dma(sp, qa[0:64, :], QT[64 * h:64 * h + 64, :], reads=b_qt, writes=[b_QA[hp]])
                    S.dma(sp, ka[0:64, :], KT[64 * h:64 * h + 64, :], reads=b_kt, writes=[b_KA[hp]])
                    S.dma(sp, TBs[hp][:], tb_d[h], writes=[b_TBs[hp]])
                    S.op(dve, lambda: V.scalar_tensor_tensor(BT[hp][:, 0, :, :], TBs[hp][:, 0, :, :], B31[:, h:h + 1], CMs[:, :, :],
                                                             ALU.subtract, ALU.add),
                         reads=[b_TBs[hp], b_B31, b_CM], writes=[b_BT[hp]])
                    S.op(dve, lambda: V.tensor_scalar(BT[hp][:, 1, :, :], TBs[hp][:, 1, :, :], B31[:, h:h + 1], None, ALU.subtract),
                         reads=[b_TBs[hp], b_B31], writes=[b_BT[hp]])
                    S.op(dve, lambda: V.tensor_reduce(KM32[:, :], ka[0:64, :].rearrange("p (n k) -> p n k", k=256), AX.X, ALU.add),
                         reads=[b_KA[hp]], writes=[b_KM32])
                    S.op(dve, lambda: V.tensor_scalar(KMb[hp][:, :], KM32[:, :], 1.0 / 256.0, None, ALU.mult),
                         reads=[b_KM32], writes=[b_KMb[hp]])
                st.append(s_load)

                def s_gate_mm():
                    bi = XB_[0]
                    bk, bb = banks[bi], b_bank[bi]
                    for qt in range(32):
                        S.op(pe, lambda qt=qt: PE.matmul(bk[:, qt * 16:(qt + 1) * 16], qa[0:64, qt * 128:(qt + 1) * 128], KMb[hp][:, :],
                                                         start=True, stop=True),
                             reads=[b_QA[hp], b_KMb[hp]], excl=[bb], inc=(qt == 31))
                    S.op(dve, lambda: V.tensor_tensor(Gs[:, :, :], bk[:, :].rearrange("p (a b) -> p a b", b=16), PMs[:, :, :], ALU.add),
                         reads=[b_PM], excl=[bb], writes=[b_Gs])
                    for qt in range(32):
                        S.op(dve, lambda qt=qt: V.max(T8[:, qt, :], Gs[:, qt, :]), reads=[b_Gs], writes=[b_T8])
                    S.op(dve, lambda: V.tensor_tensor(Ts[:, :, :], Gs[:, :, :], T8[:, :, 2:3].broadcast_to([128, 32, 16]), ALU.is_ge),
                         reads=[b_Gs, b_T8], writes=[b_Ts])
                    S.op(dve, lambda: V.scalar_tensor_tensor(MBP[:, :, 64:80], Ts[:, :, :], 1.0, BPs[:, :, :], ALU.subtract, ALU.mult),
                         reads=[b_Ts, b_BP], writes=[b_MBP])
                st.append(s_gate_mm)

                def s_transpose():
                    for grp in range(4):
                        bi = XB_[1] if grp % 2 == 0 else XB_[0]
                        bk, bb = banks[bi], b_bank[bi]
                        bkb = bk[:, :].bitcast(BF16)
                        for i in range(8):
                            qt = grp * 8 + i
                            S.op(pe, lambda qt=qt, i=i: PE.transpose(bkb[0:80, i * 128:(i + 1) * 128], MBP[:, qt, 0:80], ident[:, :]),
                                 reads=[b_MBP, b_ident], excl=[bb], inc=(i == 7))
                        S.op(dve, lambda grp=grp: V.tensor_copy(qa[64:80, grp * 1024:(grp + 1) * 1024], bkb[64:80, :]),
                             excl=[bb], writes=[b_QAm[hp]])
                st.append(s_transpose)
                return st

            def main_steps(h):
                hp = h % 2
                qa, ka = QA[hp], KA[hp]
                steps = [(j, n) for j in range(16) for n in range(j + 1)]
                return steps

            sctr = [0]
            octr = [0]

            def emit_S(h, j, n, slot):
                hp = h % 2
                qa, ka = QA[hp], KA[hp]
                bi = SB[slot % 4]
                bk, bb = banks[bi], b_bank[bi]
                near = n >= j - 1
                for kt in range(2):
                    k0 = n * 256 + kt * 128
                    S.op(pe, lambda kt=kt, k0=k0: PE.matmul(bk[:, kt * 256:(kt + 1) * 256], ka[0:80, k0:k0 + 128], qa[0:80, j * 256:(j + 1) * 256],
                                                            start=True, stop=not near),
                         reads=[b_KA[hp], b_KAo[hp], b_QA[hp], b_QAm[hp]], excl=[bb], inc=(kt == 1 and not near))
                    if near:
                        kind = 0 if n == j else 1
                        S.op(pe, lambda kt=kt, kind=kind: PE.matmul(bk[:, kt * 256:(kt + 1) * 256], ident[:, :], BT[hp][:, kind, kt, :],
                                                                    start=False, stop=True),
                             reads=[b_ident, b_BT[hp]], excl=[bb], inc=(kt == 1))
                pt, bpt = PT[slot % NPT], b_PT[slot % NPT]
                S.op(act, lambda: A.activation(pt[:, :], bk[:, :], AF.Exp), excl=[bb], writes=[bpt])

            def emit_PV(h, j, n, slot, oslot):
                pt, bpt = PT[slot % NPT], b_PT[slot % NPT]
                bi = OB[oslot % 2]
                bk, bb = banks[bi], b_bank[bi]
                for kt in range(2):
                    s_idx = n * 2 + kt
                    S.op(pe, lambda kt=kt, s_idx=s_idx: PE.matmul(bk[0:65, 0:256], VA[:, s_idx, h, 0:65], pt[:, kt * 256:(kt + 1) * 256],
                                                                  start=(n == 0 and kt == 0), stop=(n == j and kt == 1)),
                         reads=[bpt, b_VA[s_idx // 8], b_VA1], excl=[bb], inc=(kt == 1))

            def emit_norm1(h, j, oslot):
                hp = h % 2
                bi = OB[oslot % 2]
                bk, bb = banks[bi], b_bank[bi]
                os_, bos = OS[oslot % NOS], b_OS[oslot % NOS]
                rrt, brr = RR[oslot % NOS], b_RR[oslot % NOS]
                S.op(dve, lambda: V.tensor_copy(os_[:, :], bk[0:64, 0:256]), excl=[bb], writes=[bos])
                S.op(dve, lambda: V.reciprocal(rrt[64:65, :], bk[64:65, 0:256]), excl=[bb], writes=[brr])

            def emit_norm2(h, j, oslot):
                hp = h % 2
                os_, bos = OS[oslot % NOS], b_OS[oslot % NOS]
                rrt, brr = RR[oslot % NOS], b_RR[oslot % NOS]
                bi = XB_[oslot % 2]
                bk, bb = banks[bi], b_bank[bi]
                S.op(pe, lambda: PE.matmul(bk[0:64, 0:256], ones32[64:65, 0:64], rrt[64:65, :], start=True, stop=True),
                     reads=[b_ones32, brr], excl=[bb])
                S.op(dve, lambda: V.tensor_tensor(OT[hp][:, j * 256:(j + 1) * 256], os_[:, :], bk[0:64, 0:256], ALU.mult),
                     reads=[bos], excl=[bb], writes=[b_OT[hp]])

            st0 = prologue_stages(0)
            st0[0]()
            for i in range(4):
                S.dma(sp, VA[:, i * 8:(i + 1) * 8, :, :].rearrange("p s h d -> p s (h d)"),
                      VS.rearrange("(s p) f -> p s f", p=128)[:, i * 8:(i + 1) * 8, :],
                      reads=[b_vs[2 * i], b_vs[2 * i + 1]], writes=[b_VA[i]])
            for f in st0[1:]:
                f()
            for h in range(8):
                hp = h % 2
                steps = [(j, n) for j in range(16) for n in range(j + 1)]
                nxt = prologue_stages(h + 1) if h + 1 < 8 else []
                stage_at = {8: 0, 40: 1, 80: 2}
                pending_norm = []
                base = sctr[0]
                emit_S(h, steps[0][0], steps[0][1], base)
                emit_S(h, steps[1][0], steps[1][1], base + 1)
                for si, (j, n) in enumerate(steps):
                    if si + 2 < len(steps):
                        emit_S(h, steps[si + 2][0], steps[si + 2][1], base + si + 2)
                    emit_PV(h, j, n, base + si, octr[0])
                    while pending_norm and pending_norm[0][0] <= si:
                        _, jj, osl = pending_norm.pop(0)
                        emit_norm2(h, jj, osl)
                    if n == j:
                        emit_norm1(h, j, octr[0])
                        pending_norm.append((si + 5, j, octr[0]))
                        octr[0] += 1
                    if si in stage_at and nxt:
                        nxt[stage_at[si]]()
                    job_tick()
                while pending_norm:
                    _, jj, osl = pending_norm.pop(0)
                    emit_norm2(h, jj, osl)
                sctr[0] = base + len(steps)
                S.dma(sp, CAT[512 + 64 * h:512 + 64 * h + 64, :], OT[hp][:, :], reads=[b_OT[hp]], writes=[b_cath[h]])
            assert jstate["cast"] == len(jobs), (jstate, len(jobs))
        S.barrier()

    def phase_C(l, Wo, b_Wo):
        with contextlib.ExitStack() as es:
            CT = [sbt(es, f"CT{i}", [128, 8, T], BF16) for i in range(2)]; b_CT = [Buf(), Buf()]
            XT = [sbt(es, f"XTc{i}", [128, 8, T], F32) for i in range(2)]
            b_XT = [[Buf() for _ in range(8)] for _ in range(2)]
            Y = sbt(es, "Yc", [128, 8, T], F32); b_Y = [Buf() for _ in range(8)]
            SQ = sbt(es, "SQc", [128, 8, T], BF16); b_SQ = [Buf() for _ in range(8)]
            R = sbt(es, "Rc", [128, T], F32); b_R = Buf()
            src = xT if l == 0 else xs

            def load_ct(it):
                S.dma(sp, CT[it % 2][:, :, :], fm(CAT)[:, :, cols(it)], reads=[b_catp[it]] + b_cath, writes=[b_CT[it % 2]])

            def load_x(it):
                rd = [] if l == 0 else b_xs[it]
                S.dma(sp, XT[it % 2][:, :, :], fm(src)[:, :, cols(it)], reads=rd, writes=b_XT[it % 2])

            load_ct(0)
            load_x(0)
            for it in range(NT):
                if it + 1 < NT:
                    load_ct(it + 1)
                    load_x(it + 1)
                ct, bct = CT[it % 2], b_CT[it % 2]
                xt, bx = XT[it % 2], b_XT[it % 2]
                for oc in range(8):
                    oc_s = slice(oc * 128, (oc + 1) * 128)
                    bk, bb = next_bank()
                    for kc in range(8):
                        S.op(pe, lambda kc=kc: PE.matmul(bk[:, :], Wo[:, kc, oc_s], ct[:, kc, :], start=(kc == 0), stop=(kc == 7)),
                             reads=[b_Wo[kc], bct], excl=[bb], inc=(kc == 7))
                    if oc % 2 == 0:
                        S.op(dve, lambda: V.tensor_copy(Y[:, oc, :], bk[:, :]), excl=[bb], writes=[b_Y[oc]])
                    else:
                        S.op(act, lambda: A.copy(Y[:, oc, :], bk[:, :]), excl=[bb], writes=[b_Y[oc]])
                post_norm_residual(Y, b_Y, SQ, b_SQ, R, b_R, xt, bx, l, 1)
                S.dma(sp, fm(xs)[:, :, cols(it)], xt[:, :, :], reads=bx, writes=b_xs[it])
        S.barrier()

    def alloc_Wug(es, l):
        Wug = sbt(es, "Wug", [128, 8, DFF], BF16); b_Wug = [Buf() for _ in range(8)]
        load_weight(Wug, w_up[l].rearrange("(kc p) n -> p kc n", p=128)[:, :, 0:DFF], 8, b_Wug)
        return Wug, b_Wug

    def alloc_Wo(es, l):
        Wo = sbt(es, "Wo", [128, 8, DM], BF16); b_Wo = [Buf() for _ in range(8)]
        src = w_out[l].rearrange("(kc p) n -> p kc n", p=128)
        jobs = [(Wo[:, kc, :], src[:, kc, :], b_Wo[kc]) for kc in range(8)]
        return Wo, b_Wo, jobs

    def alloc_Wuv(es, l):
        Wuv = sbt(es, "Wuv", [128, 8, DFF], BF16); b_Wuv = [Buf() for _ in range(8)]
        src = w_up[l].rearrange("(kc p) n -> p kc n", p=128)
        HC = DFF // 2
        jobs = []
        for kc in range(8):
            for hf in range(2):
                jobs.append((Wuv[:, kc, hf * HC:(hf + 1) * HC], src[:, kc, DFF + hf * HC:DFF + (hf + 1) * HC], b_Wuv[kc]))
        return Wuv, b_Wuv, jobs

    def alloc_Wd(es, l):
        Wd = sbt(es, "Wd", [128, NFC, DM], BF16); b_Wd = [Buf() for _ in range(NFC)]
        load_weight(Wd, w_down[l].rearrange("(kc p) n -> p kc n", p=128), NFC, b_Wd)
        return Wd, b_Wd

    def phase_D(l, Wug, b_Wug, Wuv, b_Wuv, Wd, b_Wd):
        with contextlib.ExitStack() as es:
            Y = sbt(es, "Yd", [128, 8, T], F32); b_Y = [Buf() for _ in range(8)]
            H = sbt(es, "Hd", [128, 8, T], BF16); b_H = [Buf() for _ in range(8)]
            AT = sbt(es, "AT", [128, NFC, T], BF16); b_AT = [Buf() for _ in range(NFC)]
            SQY0 = NFC - 8
            R1 = sbt(es, "R1d", [128, T], F32); b_R1 = Buf()
            R2 = sbt(es, "R2d", [128, T], F32); b_R2 = Buf()
            NXS = 4
            XS = [sbt(es, f"XS{i}", [128, T], F32) for i in range(NXS)]; b_XS = [Buf() for _ in range(NXS)]
            xs_ctr = [0]
            Gst = [sbt(es, f"Gst{i}", [128, T + 2], F32) for i in range(2)]; b_Gst = [Buf(), Buf()]
            Cv = [sbt(es, f"Cv{i}", [128, T], F32) for i in range(2)]; b_Cv = [Buf(), Buf()]
            hist = sbt(es, "hist", [128, NFC, 2], F32); b_hist = Buf()
            S.op(pool, lambda: P.memset(hist[:, :, :], 0.0), writes=[b_hist])

            def xs_next():
                i = xs_ctr[0] % NXS
                xs_ctr[0] += 1
                return XS[i], b_XS[i]

            def xchunk(it, c):
                return fm(xs)[:, c, cols(it)]

            def ss_rstd(sq_ap, b_sq, R, b_R):
                bk, bb = next_bank()
                for c in range(8):
                    S.op(pe, lambda c=c: PE.matmul(bk[:, :], ones_b[:, :], sq_ap(c), start=(c == 0), stop=(c == 7)),
                         reads=[b_sq[c], b_ones], excl=[bb], inc=(c == 7))
                S.op(act, lambda: A.activation(R[:, :], bk[:, :], AF.Sqrt, bias=eps_t[:, 0:1], scale=1.0 / DM),
                     reads=[b_eps], excl=[bb], writes=[b_R])
                S.op(dve, lambda: V.reciprocal(R[:, :], R[:, :]), reads=[b_R], writes=[b_R])

            p1slots = {}

            def pre_loads1(it, cs):
                for c in cs:
                    xt_, bx_ = xs_next()
                    S.dma(sp, xt_[:, :], xchunk(it, c), reads=[b_xs[it][c]], writes=[bx_])
                    p1slots[(it, c)] = (xt_, bx_)

            def pre_sq1(it, cs):
                for c in cs:
                    xt_, bx_ = p1slots.pop((it, c))
                    S.op(act, lambda c=c, xt_=xt_: A.activation(H[:, c, :], xt_[:, :], AF.Square), reads=[bx_], writes=[b_H[c]])

            def pre_pass1(it):
                pre_loads1(it, range(0, 4))
                pre_sq1(it, range(0, 4))
                pre_loads1(it, range(4, 8))
                pre_sq1(it, range(4, 8))

            def pre_ss(it):
                ss_rstd(lambda c: H[:, c, :], b_H, R1, b_R1)

            def pre_pass2(it):
                for c in range(8):
                    xt_, bx_ = xs_next()
                    S.dma(sp, xt_[:, :], xchunk(it, c), reads=[b_xs[it][c]], writes=[bx_])
                    S.op(dve, lambda c=c, xt_=xt_: V.scalar_tensor_tensor(H[:, c, :], xt_[:, :], gv[:, l, 2, c:c + 1], R1[:, :], ALU.mult, ALU.mult),
                         reads=[bx_, b_gv, b_R1], writes=[b_H[c]])

            def post_sq(it):
                for c in range(8):
                    S.op(act, lambda c=c: A.activation(AT[:, SQY0 + c, :], Y[:, c, :], AF.Square),
                         reads=[b_Y[c]], writes=[b_AT[SQY0 + c]])

            def post_ss(it):
                ss_rstd(lambda c: AT[:, SQY0 + c, :], b_AT[SQY0:], R2, b_R2)

            PST = [AT[:, 16 + 2 * i:18 + 2 * i, :].rearrange("p c t -> p (c t)").bitcast(F32) for i in range(3)]
            b_PST = [[b_AT[16 + 2 * i], b_AT[17 + 2 * i]] for i in range(3)]

            def post_load(it, c):
                S.dma(sp, PST[c % 3], xchunk(it, c), reads=[b_xs[it][c]], writes=b_PST[c % 3])

            def post_piece(it, c):
                pst, bpst = PST[c % 3], b_PST[c % 3]
                S.op(dve, lambda: V.scalar_tensor_tensor(Y[:, c, :], Y[:, c, :], gv[:, l, 3, c:c + 1], R2[:, :], ALU.mult, ALU.mult),
                     reads=[b_Y[c], b_gv, b_R2], writes=[b_Y[c]])
                S.op(dve, lambda: V.tensor_tensor(pst, pst, Y[:, c, :], ALU.add),
                     reads=bpst + [b_Y[c]], writes=bpst)
                S.dma(sp, xchunk(it, c), pst, reads=bpst, writes=[b_xs[it][c]])
                if c + 3 < 8:
                    post_load(it, c + 3)

            def post_res(it):
                for c in range(3):
                    post_load(it, c)
                for c in range(8):
                    post_piece(it, c)

            pre_pass1(0)
            pre_ss(0)
            pre_pass2(0)
            for it in range(NT):
                for c in range(NFC):
                    gs, bgs = Gst[c % 2], b_Gst[c % 2]
                    cv, bcv = Cv[c % 2], b_Cv[c % 2]
                    g_s = slice(c * 128, (c + 1) * 128)
                    bg, bbg = next_bank()
                    for kc in range(8):
                        S.op(pe, lambda kc=kc: PE.matmul(bg[:, :], Wug[:, kc, g_s], H[:, kc, :], start=(kc == 0), stop=(kc == 7)),
                             reads=[b_Wug[kc], b_H[kc]], excl=[bbg], inc=(kc == 7))
                    bv, bbv = next_bank()
                    for kc in range(8):
                        S.op(pe, lambda kc=kc: PE.matmul(bv[:, :], Wuv[:, kc, g_s], H[:, kc, :], start=(kc == 0), stop=(kc == 7)),
                             reads=[b_Wuv[kc], b_H[kc]], excl=[bbv], inc=(kc == 7))
                    S.op(pool, lambda: P.tensor_copy(gs[:, 0:2], hist[:, c, :]), reads=[b_hist], writes=[bgs])
                    S.op(act, lambda: A.copy(gs[:, 2:T + 2], bg[:, :]), excl=[bbg], writes=[bgs])
                    S.op(pool, lambda: P.tensor_copy(hist[:, c, :], gs[:, T:T + 2]), reads=[bgs], writes=[b_hist])
                    S.op(act, lambda: A.activation(cv[:, :], bg[:, :], AF.Identity, bias=cb[:, l, c:c + 1], scale=cw[:, l, c, 2:3]),
                         reads=[b_cw, b_cb], excl=[bbg], writes=[bcv])
                    S.op(dve, lambda: V.scalar_tensor_tensor(cv[:, :], gs[:, 1:T + 1], cw[:, l, c, 1:2], cv[:, :], ALU.mult, ALU.add),
                         reads=[bgs, b_cw, bcv], writes=[bcv])
                    S.op(dve, lambda: V.scalar_tensor_tensor(cv[:, :], gs[:, 0:T], cw[:, l, c, 0:1], cv[:, :], ALU.mult, ALU.add),
                         reads=[bgs, b_cw, bcv], writes=[bcv])
                    S.op(act, lambda: A.activation(cv[:, :], cv[:, :], AF.Gelu_apprx_tanh), reads=[bcv], writes=[bcv])
                    S.op(dve, lambda: V.tensor_tensor(AT[:, c, :], cv[:, :], bv[:, :], ALU.mult),
                         reads=[bcv], excl=[bbv], writes=[b_AT[c]])
                    if it > 0:
                        if c == 0:
                            post_sq(it - 1)
                        elif c == 2:
                            post_ss(it - 1)
                        elif c == 4:
                            for cc in range(3):
                                post_load(it - 1, cc)
                        elif 5 <= c < 13:
                            post_piece(it - 1, c - 5)
                    if c == 16 and it + 1 < NT:
                        pre_loads1(it + 1, range(0, 4))
                if it + 1 < NT:
                    pre_sq1(it + 1, range(0, 4))
                    pre_loads1(it + 1, range(4, 8))
                    pre_sq1(it + 1, range(4, 8))
                for oc in range(8):
                    oc_s = slice(oc * 128, (oc + 1) * 128)
                    bk, bb = next_bank()
                    for kc in range(NFC):
                        S.op(pe, lambda kc=kc: PE.matmul(bk[:, :], Wd[:, kc, oc_s], AT[:, kc, :], start=(kc == 0), stop=(kc == NFC - 1)),
                             reads=[b_Wd[kc], b_AT[kc]], excl=[bb], inc=(kc == NFC - 1))
                    if oc % 2 == 0:
                        S.op(act, lambda: A.copy(Y[:, oc, :], bk[:, :]), excl=[bb], writes=[b_Y[oc]])
                    else:
                        S.op(dve, lambda: V.tensor_copy(Y[:, oc, :], bk[:, :]), excl=[bb], writes=[b_Y[oc]])
                    if it + 1 < NT:
                        if oc == 1:
                            pre_ss(it + 1)
                        elif oc == 2:
                            pre_pass2(it + 1)
            post_sq(NT - 1)
            post_ss(NT - 1)
            post_res(NT - 1)
        S.barrier()

    def phase_E(l):
        with contextlib.ExitStack() as es:
            Wg = sbt(es, "Wg", [128, 8, DM], BF16); b_Wg = [Buf() for _ in range(8)]
            Wp2 = sbt(es, "Wp2", [128, 2, DM], BF16); b_Wp2 = [Buf() for _ in range(2)]
            load_weight(Wg, w_gate[l].rearrange("(kc p) n -> p kc n", p=128), 8, b_Wg)
            load_weight(Wp2, w_ple[l].rearrange("(kc p) n -> p kc n", p=128), 2, b_Wp2)
            XBs = [sbt(es, f"XBe{i}", [128, 8, T], BF16) for i in range(2)]
            b_XBs = [[Buf() for _ in range(8)] for _ in range(2)]
            PB = [sbt(es, f"PBe{i}", [128, 2, T], BF16) for i in range(2)]; b_PB = [Buf(), Buf()]
            SG = [sbt(es, f"SG{i}", [128, T], F32) for i in range(2)]; b_SG = [Buf(), Buf()]
            TM = [sbt(es, f"TM{i}", [128, T], F32) for i in range(2)]; b_TM = [Buf(), Buf()]
            XT = [sbt(es, f"XTe{i}", [128, 8, T], F32) for i in range(3)]
            b_XT = [[Buf() for _ in range(8)] for _ in range(3)]

            def load_x(it):
                S.dma(sp, XT[it % 3][:, :, :], fm(xs)[:, :, cols(it)], reads=b_xs[it], writes=b_XT[it % 3])
            load_x(0)
            ple_load_p(l, 0, PB[0], b_PB[0])
            for it in range(NT):
                if it + 1 < NT:
                    load_x(it + 1)
                    ple_load_p(l, it + 1, PB[(it + 1) % 2], b_PB[(it + 1) % 2])
                xt, bx = XT[it % 3], b_XT[it % 3]
                ple(l, it, xt, bx, XBs[it % 2], b_XBs[it % 2], PB[it % 2], b_PB[it % 2], Wg, b_Wg, Wp2, b_Wp2, SG, b_SG, TM, b_TM)
                S.dma(sp, fm(outT)[:, :, cols(it)], xt[:, :, :], reads=bx, writes=[b_out[it]])
        S.barrier()

    def run_all():
        for l in layers:
            phase_A(l)
            if stop_after == ("A", l):
                return
            with contextlib.ExitStack() as esW:
                Wuv, b_Wuv, jobs = alloc_Wuv(esW, l)
                with contextlib.ExitStack() as esO:
                    Wo, b_Wo, jobs_o = alloc_Wo(esO, l)
                    phase_B(l, jobs_o + jobs)
                    if stop_after == ("B", l):
                        return
                    phase_C(l, Wo, b_Wo)
                    if stop_after == ("C", l):
                        return
                Wug, b_Wug = alloc_Wug(esW, l)
                Wd, b_Wd = alloc_Wd(esW, l)
                phase_D(l, Wug, b_Wug, Wuv, b_Wuv, Wd, b_Wd)
                if stop_after == ("D", l):
                    return
        phase_E(layers[-1])
    run_all()
    S.finish()
    es0.close()
    S.close()
    return S
import math
import ml_dtypes
from concourse.bass_utils import run_bass_kernel_spmd

_BF = ml_dtypes.bfloat16


def _rel_bucket_np(n):
    n = np.maximum(n, 0)
    max_exact = 16
    nf = np.maximum(n, 1).astype(np.float32)
    large = max_exact + (np.log(nf / np.float32(max_exact)) / np.float32(math.log(128 / max_exact))
                         * np.float32(32 - max_exact)).astype(np.int32)
    large = np.minimum(large, 31)
    return np.where(n < max_exact, n, large)


def _static_consts():
    c = {}
    c["ident"] = np.eye(128, dtype=np.float32).astype(_BF)
    oh = np.zeros((16, SEQ), np.float32)
    for n in range(16):
        oh[n, n * 256:(n + 1) * 256] = 1.0
    c["onehot"] = oh.astype(_BF)
    pm = np.zeros((32, 16), np.float32)
    bp = np.zeros((32, 16), np.float32)
    for qt in range(32):
        j = qt // 2
        pm[qt, j:] = -BIG
        bp[qt, :j] = BIG
    c["pmall"] = np.ascontiguousarray(np.broadcast_to(pm, (128, 32, 16)))
    c["bigpast"] = np.ascontiguousarray(np.broadcast_to(bp, (128, 32, 16)))
    k = np.arange(256)[:, None]
    q = np.arange(256)[None, :]
    cm = np.where(q - k >= 0, 0.0, -BIG).astype(np.float32)
    c["cm"] = np.ascontiguousarray(cm.reshape(2, 128, 256).transpose(1, 0, 2))
    rc = np.zeros((4, 16), np.float32)
    for g, w in enumerate((2, 4, 8, 16)):
        rc[g] = 1.0 / np.minimum(np.arange(16) + 1, w)
    c["rc"] = np.ascontiguousarray(np.broadcast_to(rc, (128, 4, 16)))
    return c


def _prep_shared(inp):
    f = lambda a: np.ascontiguousarray(np.asarray(a, dtype=np.float32))
    d = {}
    for k_src, k_dst in (("w_in", "w_in"), ("w_pool", "w_pool"), ("w_out", "w_out"), ("w_up", "w_up"),
                         ("w_down", "w_down"), ("w_ple", "w_ple"), ("w_ple_gate", "w_gate")):
        d[k_dst] = f(inp[k_src])
    g = np.stack([f(inp["g_mix_pre"]), f(inp["g_mix_post"]), f(inp["g_ffn_pre"]), f(inp["g_ffn_post"])], axis=1)
    d["gvec"] = np.ascontiguousarray(g.reshape(2, 4, 8, 128).transpose(3, 0, 1, 2))
    d["pscale"] = np.ascontiguousarray(f(inp["pool_scale"]).reshape(2, 4, 128).transpose(2, 0, 1))
    d["convw"] = np.ascontiguousarray(f(inp["conv_w"]).reshape(2, 3, NFC, 128).transpose(3, 0, 2, 1))
    d["convb"] = np.ascontiguousarray(f(inp["conv_b"]).reshape(2, NFC, 128).transpose(2, 0, 1))
    rb = f(inp["rel_bias"])
    k = np.arange(256)[:, None]
    q = np.arange(256)[None, :]
    idx_own = _rel_bucket_np(q - k)
    idx_prev = _rel_bucket_np(q + 256 - k)
    tb = np.stack([rb[idx_own], rb[idx_prev]], axis=0)
    tb = tb.reshape(2, 2, 128, 256, 8).transpose(4, 2, 0, 1, 3)
    d["tb"] = np.ascontiguousarray(tb)
    d["b31"] = np.ascontiguousarray(np.broadcast_to(rb[31][None, :], (128, 8)))
    d.update(_static_consts())
    return d


_CACHE = {}


def _get_nc(debug=False, stop_after=None, layers=(0, 1)):
    key = (debug, stop_after, layers)
    if key not in _CACHE:
        nc = bass.Bass("TRN2", target_bir_lowering=False)
        build_program(nc, debug=debug, stop_after=stop_after, layers=layers)
        _CACHE[key] = nc
    return _CACHE[key]


def kernel(**inputs):
    x = np.asarray(inputs["x"], dtype=np.float32)
    p = np.asarray(inputs["p"], dtype=np.float32)
    shared = _prep_shared(inputs)
    n = x.shape[0]
    in_maps = []
    for b in range(n):
        m = dict(shared)
        m["xT"] = np.ascontiguousarray(x[b].T)
        m["pT"] = np.ascontiguousarray(p[:, b].transpose(0, 2, 1))
        in_maps.append(m)
    nc = _get_nc()
    res = run_bass_kernel_spmd(nc, in_maps, core_ids=list(range(n)))
    out = np.stack([np.asarray(r["outT"], dtype=np.float32).T for r in res.results], axis=0)
    return np.ascontiguousarray(out)
```

```python
import numpy as np
import concourse.bass as bass
import concourse.mybir as mybir

F32 = mybir.dt.float32
BF16 = mybir.dt.bfloat16
AF = mybir.ActivationFunctionType
ALU = mybir.AluOpType
AX = mybir.AxisListType

EPOCH = 30000
NDMA_SEMS = 8


class Tok:
    __slots__ = ("src", "sem", "val")

    def __init__(self, src, sem, val):
        self.src = src
        self.sem = sem
        self.val = val


class Buf:
    __slots__ = ("name", "w", "r")

    def __init__(self, name=""):
        self.name = name
        self.w = None
        self.r = []


class Queue:
    def __init__(self, S, name, obj, is_pe=False):
        self.S = S
        self.name = name
        self.obj = obj
        self.is_pe = is_pe
        self.seen = {}
        self.sems = []
        self.count = 0
        self.pending = None
        self.last = None
        self.dma_sems = []
        self.dma_k = 0
        self.dma_toks = []

    def _cur_sem(self):
        if not self.sems or self.count >= EPOCH:
            self.sems.append(self.S.new_sem(f"{self.name}_e{len(self.sems)}"))
            self.count = 0
        return self.sems[-1]

    def wait(self, tok):
        key = id(tok.sem)
        if self.seen.get(key, 0) >= tok.val:
            return
        self.obj.wait_ge(tok.sem, tok.val)
        self.seen[key] = tok.val
        self.S.nwaits += 1


class Sched:
    def __init__(self, nc):
        self.nc = nc
        self._sem_ctx = []
        self.nwaits = 0
        self.ninstr = 0
        self.pe = Queue(self, "pe", nc.tensor, is_pe=True)
        self.act = Queue(self, "act", nc.scalar)
        self.dve = Queue(self, "dve", nc.vector)
        self.pool = Queue(self, "pool", nc.gpsimd)
        self.sp = Queue(self, "sp", nc.sync)
        self.queues = [self.pe, self.act, self.dve, self.pool, self.sp]

    def new_sem(self, name):
        ctx = self.nc.semaphore(name)
        sem = ctx.__enter__()
        self._sem_ctx.append(ctx)
        return sem

    def close(self):
        for ctx in reversed(self._sem_ctx):
            ctx.__exit__(None, None, None)
        self._sem_ctx = []

    def op(self, q, fn, reads=(), writes=(), excl=(), inc=True):
        waits = []
        for b in reads:
            if b.w is not None and not (q.is_pe and b.w.src is q):
                waits.append(b.w)
        for b in writes:
            if b.w is not None and b.w.src is not q:
                waits.append(b.w)
            for t in b.r:
                if t.src is not q:
                    waits.append(t)
        for b in excl:
            if b.w is not None and b.w.src is not q:
                waits.append(b.w)
            for t in b.r:
                if t.src is not q:
                    waits.append(t)
        for t in waits:
            q.wait(t)
        ins = fn()
        self.ninstr += 1
        if q.pending is None:
            sem = q._cur_sem()
            q.pending = Tok(q, sem, q.count + 1)
        tok = q.pending
        if inc:
            ins.then_inc(tok.sem, 1)
            q.count += 1
            q.pending = None
            q.last = tok
        for b in reads:
            b.r = [t for t in b.r if t.src is not q]
            b.r.append(tok)
        for b in writes:
            b.w = tok
            b.r = []
        for b in excl:
            b.w = tok
            b.r = []
        return tok

    def dma(self, q, out, in_, reads=(), writes=(), **kw):
        waits = []
        for b in reads:
            if b.w is not None:
                waits.append(b.w)
        for b in writes:
            if b.w is not None:
                waits.append(b.w)
            waits.extend(b.r)
        for t in waits:
            q.wait(t)
        k = q.dma_k
        if len(q.dma_sems) < NDMA_SEMS:
            q.dma_sems.append(self.new_sem(f"{q.name}_dma{len(q.dma_sems)}"))
            q.dma_toks.append(None)
        slot = k % NDMA_SEMS
        prev = q.dma_toks[slot]
        if prev is not None:
            q.wait(prev)
        sem = q.dma_sems[slot]
        val = 16 * (k // NDMA_SEMS + 1)
        ins = q.obj.dma_start(out=out, in_=in_, **kw)
        ins.then_inc(sem, 16)
        self.ninstr += 1
        tok = Tok(("dma", q.name), sem, val)
        q.dma_toks[slot] = tok
        q.dma_k += 1
        for b in reads:
            b.r.append(tok)
        for b in writes:
            b.w = tok
            b.r = []
        return tok

    def barrier(self):
        toks = []
        for q in self.queues:
            assert q.pending is None, f"pending non-inc'd instruction on {q.name}"
            if q.last is not None:
                toks.append(q.last)
            for t in q.dma_toks:
                if t is not None:
                    toks.append(t)
        for q in self.queues:
            for t in toks:
                if t.src is q:
                    continue
                q.wait(t)

    def finish(self, q=None):
        q = q or self.sp
        toks = []
        for qq in self.queues:
            if qq.last is not None and qq is not q:
                toks.append(qq.last)
            for t in qq.dma_toks:
                if t is not None:
                    toks.append(t)
        for t in toks:
            q.wait(t)
import contextlib

SEQ = 4096
DM = 1024
T = 512
NT = SEQ // T
DFF = 2816
NFC = DFF // 128
BIG = 30000.0
EPS = 1e-6


def build_program(nc, debug=False, stop_after=None, layers=(0, 1)):
    S = Sched(nc)
    kind_s = "ExternalOutput" if debug else "Internal"

    def din(name, shape, dt=F32):
        return nc.dram_tensor(name, list(shape), dt, kind="ExternalInput").ap()

    xT = din("xT", [DM, SEQ])
    pT = din("pT", [2, 256, SEQ])
    w_in = din("w_in", [2, DM, 2048])
    w_pool = din("w_pool", [2, 4, 128, 128])
    w_out = din("w_out", [2, DM, DM])
    w_up = din("w_up", [2, DM, 2 * DFF])
    w_down = din("w_down", [2, DFF, DM])
    w_ple = din("w_ple", [2, 256, DM])
    w_gate = din("w_gate", [2, DM, DM])
    gvec = din("gvec", [128, 2, 4, 8])
    pscale = din("pscale", [128, 2, 4])
    convw = din("convw", [128, 2, NFC, 3])
    convb = din("convb", [128, 2, NFC])
    ident_d = din("ident", [128, 128], BF16)
    onehot_d = din("onehot", [16, SEQ], BF16)
    pmall_d = din("pmall", [128, 32, 16])
    bigpast_d = din("bigpast", [128, 32, 16])
    tb_d = din("tb", [8, 128, 2, 2, 256])
    cm_d = din("cm", [128, 2, 256])
    b31_d = din("b31", [128, 8])
    rc_d = din("rc", [128, 4, 16])

    outT = nc.dram_tensor("outT", [DM, SEQ], F32, kind="ExternalOutput").ap()
    xs = nc.dram_tensor("xs", [DM, SEQ], F32, kind=kind_s).ap()
    QT = nc.dram_tensor("QT", [512, SEQ], BF16, kind=kind_s).ap()
    KT = nc.dram_tensor("KT", [512, SEQ], BF16, kind=kind_s).ap()
    VS = nc.dram_tensor("VS", [SEQ, 520], BF16, kind=kind_s).ap()
    CAT = nc.dram_tensor("CAT", [DM, SEQ], BF16, kind=kind_s).ap()

    b_xs = [[Buf(f"xs{i}_{c}") for c in range(8)] for i in range(NT)]
    b_qt = [Buf() for _ in range(NT)]
    b_kt = [Buf() for _ in range(NT)]
    b_vs = [Buf() for _ in range(NT)]
    b_catp = [Buf() for _ in range(NT)]
    b_cath = [Buf() for _ in range(8)]
    b_out = [Buf() for _ in range(NT)]

    pe, act, dve, pool, sp = S.pe, S.act, S.dve, S.pool, S.sp
    V, A, P, PE = nc.vector, nc.scalar, nc.gpsimd, nc.tensor

    es0 = contextlib.ExitStack()

    uid = [0]

    def sbt(es, name, shape, dt):
        uid[0] += 1
        return es.enter_context(nc.sbuf_tensor(f"sb{uid[0]}_{name}", list(shape), dt))

    banks = [es0.enter_context(nc.psum_tensor(f"bank{i}", [128, 512], F32)) for i in range(8)]
    b_bank = [Buf(f"bank{i}") for i in range(8)]
    rr = [0]

    def next_bank(lo=0, hi=8):
        i = lo + rr[0] % (hi - lo)
        rr[0] += 1
        return banks[i], b_bank[i]

    ident = sbt(es0, "ident", [128, 128], BF16); b_ident = Buf()
    ones_b = sbt(es0, "ones_b", [128, 128], BF16); b_ones = Buf()
    ones32 = sbt(es0, "ones32", [128, 64], F32); b_ones32 = Buf()
    gv = sbt(es0, "gv", [128, 2, 4, 8], F32); b_gv = Buf()
    psc = sbt(es0, "psc", [128, 2, 4], F32); b_psc = Buf()
    cw = sbt(es0, "cw", [128, 2, NFC, 3], F32); b_cw = Buf()
    cb = sbt(es0, "cb", [128, 2, NFC], F32); b_cb = Buf()
    S.dma(sp, ident[:], ident_d, writes=[b_ident])
    S.dma(sp, gv[:], gvec, writes=[b_gv])
    S.dma(sp, psc[:], pscale, writes=[b_psc])
    S.dma(sp, cw[:], convw, writes=[b_cw])
    S.dma(sp, cb[:], convb, writes=[b_cb])
    S.op(dve, lambda: V.memset(ones_b[:], 1.0), writes=[b_ones])
    S.op(dve, lambda: V.memset(ones32[:], 1.0), writes=[b_ones32])

    def cols(it):
        return slice(it * T, (it + 1) * T)

    def fm(ap):
        return ap.rearrange("(c p) t -> p c t", p=128)

    def load_weight(dst, src_view, nchunk, bufs):
        for kc in range(nchunk):
            S.dma(pool, dst[:, kc, :], src_view[:, kc, :], writes=[bufs[kc]])

    def rms_rstd(src, b_src, sq, b_sq, R, b_R):
        for c in range(8):
            S.op(act, lambda c=c: A.activation(sq[:, c, :], src[:, c, :], AF.Square),
                 reads=[b_src[c]], writes=[b_sq[c]])
        bk, bb = next_bank()
        for c in range(8):
            S.op(pe, lambda c=c: PE.matmul(bk[:, :], ones_b[:, :], sq[:, c, :], start=(c == 0), stop=(c == 7)),
                 reads=[b_sq[c], b_ones], excl=[bb], inc=(c == 7))
        S.op(act, lambda: A.activation(R[:, :], bk[:, :], AF.Sqrt, bias=eps_t[:, 0:1], scale=1.0 / DM),
             reads=[b_eps], excl=[bb], writes=[b_R])
        S.op(dve, lambda: V.reciprocal(R[:, :], R[:, :]), reads=[b_R], writes=[b_R])

    eps_t = sbt(es0, "eps_t", [128, 1], F32); b_eps = Buf()
    S.op(dve, lambda: V.memset(eps_t[:], EPS), writes=[b_eps])

    def pre_norm(xt, b_x, H, b_H, R, b_R, l, which):
        rms_rstd(xt, b_x, H, b_H, R, b_R)
        for c in range(8):
            q, E = dve, V
            S.op(q, lambda c=c, E=E: E.scalar_tensor_tensor(
                H[:, c, :], xt[:, c, :], gv[:, l, which, c:c + 1], R[:, :], ALU.mult, ALU.mult),
                reads=[b_x[c], b_gv, b_R], writes=[b_H[c]])

    def post_norm_residual(Y, b_Y, sq, b_sq, R, b_R, xt, b_x, l, which):
        rms_rstd(Y, b_Y, sq, b_sq, R, b_R)
        for c in range(8):
            S.op(dve, lambda c=c: V.scalar_tensor_tensor(
                Y[:, c, :], Y[:, c, :], gv[:, l, which, c:c + 1], R[:, :], ALU.mult, ALU.mult),
                reads=[b_Y[c], b_gv, b_R], writes=[b_Y[c]])
            S.op(pool, lambda c=c: P.tensor_tensor(xt[:, c, :], xt[:, c, :], Y[:, c, :], ALU.add),
                 reads=[b_x[c], b_Y[c]], writes=[b_x[c]])

    def ple_load_p(l, it, PB, b_PB):
        S.dma(pool, PB[:, :, :], fm(pT[l])[:, :, cols(it)], writes=[b_PB])

    def ple(l, it, xt, b_x, XB, b_XB, PB, b_PB, Wg, b_Wg, Wp2, b_Wp2, SG, b_SG, TM, b_TM):
        for c in range(8):
            S.op(act, lambda c=c: A.copy(XB[:, c, :], xt[:, c, :]), reads=[b_x[c]], writes=[b_XB[c]])
        for oc in range(8):
            oc_s = slice(oc * 128, (oc + 1) * 128)
            bg, bbg = next_bank()
            for kc in range(8):
                S.op(pe, lambda kc=kc: PE.matmul(bg[:, :], Wg[:, kc, oc_s], XB[:, kc, :], start=(kc == 0), stop=(kc == 7)),
                     reads=[b_Wg[kc], b_XB[kc]], excl=[bbg], inc=(kc == 7))
            sg, bsg = SG[oc % 2], b_SG[oc % 2]
            S.op(act, lambda: A.activation(sg[:, :], bg[:, :], AF.Sigmoid), excl=[bbg], writes=[bsg])
            be, bbe = next_bank()
            for kc in range(2):
                S.op(pe, lambda kc=kc: PE.matmul(be[:, :], Wp2[:, kc, oc_s], PB[:, kc, :], start=(kc == 0), stop=(kc == 1)),
                     reads=[b_Wp2[kc], b_PB], excl=[bbe], inc=(kc == 1))
            tm, btm = TM[oc % 2], b_TM[oc % 2]
            S.op(dve, lambda: V.tensor_tensor(tm[:, :], be[:, :], sg[:, :], ALU.mult),
                 reads=[bsg], excl=[bbe], writes=[btm])
            S.op(dve, lambda oc=oc: V.tensor_tensor(xt[:, oc, :], xt[:, oc, :], tm[:, :], ALU.add),
                 reads=[btm, b_x[oc]], writes=[b_x[oc]])

    def phase_A(l):
        with contextlib.ExitStack() as es:
            Win = sbt(es, "Win", [128, 8, 2048], BF16); b_Win = [Buf() for _ in range(8)]
            Wpl = sbt(es, "Wpl", [128, 4, 128], BF16); b_Wpl = [Buf()]
            load_weight(Win, w_in[l].rearrange("(kc p) n -> p kc n", p=128), 8, b_Win)
            S.dma(pool, Wpl[:, :, :], w_pool[l].rearrange("g c d -> c g d"), writes=b_Wpl)
            if l > 0:
                Wg = sbt(es, "Wg", [128, 8, DM], BF16); b_Wg = [Buf() for _ in range(8)]
                Wp2 = sbt(es, "Wp2", [128, 2, DM], BF16); b_Wp2 = [Buf() for _ in range(2)]
                load_weight(Wg, w_gate[l - 1].rearrange("(kc p) n -> p kc n", p=128), 8, b_Wg)
                load_weight(Wp2, w_ple[l - 1].rearrange("(kc p) n -> p kc n", p=128), 2, b_Wp2)
                XBs = [sbt(es, f"XB{i}", [128, 8, T], BF16) for i in range(2)]
                b_XBs = [[Buf() for _ in range(8)] for _ in range(2)]
                PBs = [sbt(es, f"PB{i}", [128, 2, T], BF16) for i in range(2)]; b_PBs = [Buf(), Buf()]
                SG = [sbt(es, f"SG{i}", [128, T], F32) for i in range(2)]; b_SG = [Buf(), Buf()]
                TM = [sbt(es, f"TM{i}", [128, T], F32) for i in range(2)]; b_TM = [Buf(), Buf()]
            NXT = 3
            XT = [sbt(es, f"XT{i}", [128, 8, T], F32) for i in range(NXT)]
            b_XT = [[Buf() for _ in range(8)] for _ in range(NXT)]
            Hs = [sbt(es, f"H{i}", [128, 8, T], BF16) for i in range(2)]
            b_Hs = [[Buf() for _ in range(8)] for _ in range(2)]
            R = sbt(es, "R", [128, T], F32); b_R = Buf()
            U = sbt(es, "U", [128, 4, 16 + T], F32); b_U = [Buf() for _ in range(4)]
            TA = sbt(es, "TA", [128, 16 + T], F32); b_TA = Buf()
            TB_ = sbt(es, "TB_", [128, 16 + T], F32); b_TB = Buf()
            T16 = sbt(es, "T16", [128, 16], F32); b_T16 = Buf()
            RC = sbt(es, "RC", [128, 4, 16], F32); b_RC = Buf()
            PLs = [sbt(es, f"PL{i}", [128, 4, T], BF16) for i in range(2)]
            b_PLs = [[Buf() for _ in range(4)] for _ in range(2)]
            QS = [sbt(es, f"QS{i}", [128, 4, T], BF16) for i in range(2)]; b_QS = [Buf(), Buf()]
            KS = [sbt(es, f"KS{i}", [128, 4, T], BF16) for i in range(2)]; b_KS = [Buf(), Buf()]
            VSt = [sbt(es, f"VSt{i}", [128, 4, 8, 65], BF16) for i in range(2)]; b_VSt = [Buf(), Buf()]
            for i in range(2):
                S.op(pool, lambda i=i: P.memset(VSt[i][:, :, :, 64:65], 1.0), writes=[b_VSt[i]])
            CP = [sbt(es, f"CP{i}", [128, 4, T], BF16) for i in range(2)]; b_CP = [Buf(), Buf()]
            S.dma(sp, RC[:], rc_d, writes=[b_RC])
            S.op(pool, lambda: P.memset(U[:, :, 0:16], 0.0), writes=b_U)
            src = xT if l == 0 else xs
            W = 16 + T

            def load_x(it):
                xt, bx = XT[it % NXT], b_XT[it % NXT]
                rd = [] if l == 0 else b_xs[it]
                S.dma(sp, xt[:, :, :], fm(src)[:, :, cols(it)], reads=rd, writes=bx)

            def stage1a(it):
                xt, bx = XT[it % NXT], b_XT[it % NXT]
                H, b_H = Hs[it % 2], b_Hs[it % 2]
                if l > 0:
                    if it + 1 < NT:
                        ple_load_p(l - 1, it + 1, PBs[(it + 1) % 2], b_PBs[(it + 1) % 2])
                    ple(l - 1, it, xt, bx, XBs[it % 2], b_XBs[it % 2], PBs[it % 2], b_PBs[it % 2],
                        Wg, b_Wg, Wp2, b_Wp2, SG, b_SG, TM, b_TM)
                    S.dma(sp, fm(xs)[:, :, cols(it)], xt[:, :, :], reads=bx, writes=b_xs[it])
                for c in range(8):
                    S.op(act, lambda c=c: A.activation(H[:, c, :], xt[:, c, :], AF.Square), reads=[bx[c]], writes=[b_H[c]])

            def stage1b(it):
                xt, bx = XT[it % NXT], b_XT[it % NXT]
                H, b_H = Hs[it % 2], b_Hs[it % 2]
                bk, bb = next_bank()
                for c in range(8):
                    S.op(pe, lambda c=c: PE.matmul(bk[:, :], ones_b[:, :], H[:, c, :], start=(c == 0), stop=(c == 7)),
                         reads=[b_H[c], b_ones], excl=[bb], inc=(c == 7))
                S.op(act, lambda: A.activation(R[:, :], bk[:, :], AF.Sqrt, bias=eps_t[:, 0:1], scale=1.0 / DM),
                     reads=[b_eps], excl=[bb], writes=[b_R])
                S.op(dve, lambda: V.reciprocal(R[:, :], R[:, :]), reads=[b_R], writes=[b_R])
                for c in range(8):
                    S.op(dve, lambda c=c: V.scalar_tensor_tensor(H[:, c, :], xt[:, c, :], gv[:, l, 0, c:c + 1], R[:, :], ALU.mult, ALU.mult),
                         reads=[bx[c], b_gv, b_R], writes=[b_H[c]])

            def proj_group(it, oc):
                H, b_H = Hs[it % 2], b_Hs[it % 2]
                par = it % 2
                oc_s = slice(oc * 128, (oc + 1) * 128)
                bk, bb = next_bank()
                for kc in range(8):
                    S.op(pe, lambda kc=kc: PE.matmul(bk[:, :], Win[:, kc, oc_s], H[:, kc, :], start=(kc == 0), stop=(kc == 7)),
                         reads=[b_Win[kc], b_H[kc]], excl=[bb], inc=(kc == 7))
                if oc < 4:
                    S.op(act, lambda: A.copy(U[:, oc, 16:16 + T], bk[:, :]), excl=[bb], writes=[b_U[oc]])
                elif oc < 8:
                    S.op(act, lambda: A.mul(QS[par][:, oc - 4, :], bk[:, :], 0.125), excl=[bb], writes=[b_QS[par]])
                    if oc == 7:
                        S.dma(sp, fm(QT)[:, :, cols(it)], QS[par][:, :, :], reads=[b_QS[par]], writes=[b_qt[it]])
                else:
                    S.op(dve, lambda: V.tensor_copy(KS[par][:, oc - 8, :], bk[:, :]), excl=[bb], writes=[b_KS[par]])
                    if oc == 11:
                        S.dma(sp, fm(KT)[:, :, cols(it)], KS[par][:, :, :], reads=[b_KS[par]], writes=[b_kt[it]])

            def proj_v(it):
                H, b_H = Hs[it % 2], b_Hs[it % 2]
                par = it % 2
                for sub in range(4):
                    bk, bb = next_bank()
                    for kc in range(8):
                        S.op(pe, lambda kc=kc: PE.matmul(bk[:, :], H[:, kc, sub * 128:(sub + 1) * 128], Win[:, kc, 1536:2048],
                                                         start=(kc == 0), stop=(kc == 7)),
                             reads=[b_Win[kc], b_H[kc]], excl=[bb], inc=(kc == 7))
                    bkv = bk[:, :].rearrange("p (h d) -> p h d", h=8)
                    if sub % 2 == 0:
                        S.op(dve, lambda: V.tensor_copy(VSt[par][:, sub, :, 0:64], bkv), excl=[bb], writes=[b_VSt[par]])
                    else:
                        S.op(act, lambda: A.copy(VSt[par][:, sub, :, 0:64], bkv), excl=[bb], writes=[b_VSt[par]])
                S.dma(sp, VS.rearrange("(s p) f -> p s f", p=128)[:, it * 4:(it + 1) * 4, :],
                      VSt[par][:, :, :, :].rearrange("p s h d -> p s (h d)"),
                      reads=[b_VSt[par]], writes=[b_vs[it]])

            def poolE(it):
                PL, b_PL = PLs[it % 2], b_PLs[it % 2]
                for g in range(4):
                    Ug = U[:, g, :]
                    S.op(pool, lambda: P.tensor_tensor(TA[:, 1:W], Ug[:, 1:W], Ug[:, 0:W - 1], ALU.add),
                         reads=[b_U[g]], writes=[b_TA])
                    fin, bfin = TA, b_TA
                    if g >= 1:
                        S.op(pool, lambda: P.tensor_tensor(TB_[:, 3:W], TA[:, 3:W], TA[:, 1:W - 2], ALU.add),
                             reads=[b_TA], writes=[b_TB])
                        fin, bfin = TB_, b_TB
                    if g >= 2:
                        S.op(pool, lambda: P.tensor_tensor(TA[:, 7:W], TB_[:, 7:W], TB_[:, 3:W - 4], ALU.add),
                             reads=[b_TB], writes=[b_TA])
                        fin, bfin = TA, b_TA
                    if g >= 3:
                        S.op(pool, lambda: P.tensor_tensor(TB_[:, 15:W], TA[:, 15:W], TA[:, 7:W - 8], ALU.add),
                             reads=[b_TA], writes=[b_TB])
                        fin, bfin = TB_, b_TB
                    wsz = float(2 ** (g + 1))
                    S.op(dve, lambda: V.scalar_tensor_tensor(PL[:, g, :], fin[:, 16:W], 1.0 / wsz, Ug[:, 16:W], ALU.mult, ALU.subtract),
                         reads=[bfin, b_U[g]], writes=[b_PL[g]])
                    if it == 0:
                        S.op(pool, lambda: P.tensor_tensor(T16[:, :], fin[:, 16:32], RC[:, g, :], ALU.mult),
                             reads=[bfin, b_RC], writes=[b_T16])
                        S.op(pool, lambda: P.tensor_tensor(PL[:, g, 0:16], T16[:, :], Ug[:, 16:32], ALU.subtract),
                             reads=[b_T16, b_U[g]], writes=[b_PL[g]])
                    S.op(pool, lambda: P.tensor_copy(U[:, g, 0:16], U[:, g, T:T + 16]), reads=[b_U[g]], writes=[b_U[g]])

            def poolM(it):
                PL, b_PL = PLs[it % 2], b_PLs[it % 2]
                par = it % 2
                for g in range(4):
                    bk, bb = next_bank()
                    S.op(pe, lambda: PE.matmul(bk[:, :], Wpl[:, g, :], PL[:, g, :], start=True, stop=True),
                         reads=[b_Wpl[0], b_PL[g]], excl=[bb])
                    S.op(act, lambda: A.mul(CP[par][:, g, :], bk[:, :], psc[:, l, g:g + 1]),
                         reads=[b_psc], excl=[bb], writes=[b_CP[par]])
                S.dma(sp, fm(CAT)[:, 0:4, cols(it)], CP[par][:, :, :], reads=[b_CP[par]], writes=[b_catp[it]])

            load_x(0)
            load_x(1)
            if l > 0:
                ple_load_p(l - 1, 0, PBs[0], b_PBs[0])
            stage1a(0)
            stage1b(0)
            for it in range(NT):
                if it + 2 < NT:
                    load_x(it + 2)
                if it + 1 < NT:
                    stage1a(it + 1)
                for oc in range(0, 6):
                    proj_group(it, oc)
                if it + 1 < NT:
                    stage1b(it + 1)
                for oc in range(6, 12):
                    proj_group(it, oc)
                proj_v(it)
                if it > 0:
                    poolM(it - 1)
                poolE(it)
            poolM(NT - 1)
        S.barrier()

    def phase_B(l, jobs=()):
        with contextlib.ExitStack() as es:
            jobs = list(jobs)
            WST = [sbt(es, f"WST{i}", [128, DFF // 2], F32) for i in range(2)]; b_WST = [Buf(), Buf()]
            jstate = {"dma": 0, "cast": 0, "g": 0}

            def job_tick():
                g = jstate["g"]
                jstate["g"] += 1
                if not jobs:
                    return
                k = jstate["dma"]
                if k < len(jobs) and g == 20 + 40 * k:
                    n = jobs[k][1].shape[-1]
                    S.dma(sp, WST[k % 2][:, 0:n], jobs[k][1], writes=[b_WST[k % 2]])
                    jstate["dma"] += 1
                k = jstate["cast"]
                if k < len(jobs) and g == 20 + 40 * k + 24:
                    dst, src_, bdst = jobs[k]
                    n = src_.shape[-1]
                    S.op(dve, lambda: V.tensor_copy(dst, WST[k % 2][:, 0:n]), reads=[b_WST[k % 2]], writes=[bdst])
                    jstate["cast"] += 1

            QA = [sbt(es, f"QA{i}", [128, SEQ], BF16) for i in range(2)]
            KA = [sbt(es, f"KA{i}", [128, SEQ], BF16) for i in range(2)]
            b_QA = [Buf(), Buf()]; b_QAm = [Buf(), Buf()]; b_KA = [Buf(), Buf()]; b_KAo = [Buf(), Buf()]
            VA = sbt(es, "VA", [128, 32, 8, 65], BF16); b_VA = [Buf() for _ in range(4)]; b_VA1 = Buf()
            OT1 = sbt(es, "OT", [64, SEQ], BF16); OT = [OT1, OT1]; b_OT1 = Buf(); b_OT = [b_OT1, b_OT1]
            PMs = sbt(es, "PMs", [128, 32, 16], F32); b_PM = Buf()
            BPs = sbt(es, "BPs", [128, 32, 16], F32); b_BP = Buf()
            CMs = sbt(es, "CMs", [128, 2, 256], F32); b_CM = Buf()
            B31 = sbt(es, "B31", [128, 8], F32); b_B31 = Buf()
            TBs1 = sbt(es, "TBs", [128, 2, 2, 256], F32); TBs = [TBs1, TBs1]; b_TBs1 = Buf(); b_TBs = [b_TBs1, b_TBs1]
            BT = [sbt(es, f"BT{i}", [128, 2, 2, 256], BF16) for i in range(2)]; b_BT = [Buf(), Buf()]
            KM32 = sbt(es, "KM32", [64, 16], F32); b_KM32 = Buf()
            KMb = [sbt(es, f"KMb{i}", [64, 16], BF16) for i in range(2)]; b_KMb = [Buf(), Buf()]
            Gs = sbt(es, "Gs", [128, 32, 16], F32); b_Gs = Buf()
            T8 = sbt(es, "T8", [128, 32, 8], F32); b_T8 = Buf()
            Ts = sbt(es, "Ts", [128, 32, 16], F32); b_Ts = Buf()
            MBP = sbt(es, "MBP", [128, 32, 80], BF16); b_MBP = Buf()
            NPT = 4
            PT = [sbt(es, f"PT{i}", [128, 512], BF16) for i in range(NPT)]; b_PT = [Buf() for _ in range(NPT)]
            NOS = 4
            OS = [sbt(es, f"OS{i}", [64, 256], F32) for i in range(NOS)]; b_OS = [Buf() for _ in range(NOS)]
            RR = [sbt(es, f"RR{i}", [128, 256], F32) for i in range(NOS)]; b_RR = [Buf() for _ in range(NOS)]

            S.dma(sp, PMs[:], pmall_d, writes=[b_PM])
            S.dma(sp, BPs[:], bigpast_d, writes=[b_BP])
            S.dma(sp, CMs[:], cm_d, writes=[b_CM])
            S.dma(sp, B31[:], b31_d, writes=[b_B31])
            for i in range(2):
                S.dma(sp, KA[i][64:80, :], onehot_d, writes=[b_KAo[i]])
            S.op(pool, lambda: P.memset(MBP[:, :, :], 0.0), writes=[b_MBP])

            SB = [0, 1, 2, 3]
            OB = [4, 5]
            XB_ = [6, 7]

            def prologue_stages(h):
                hp = h % 2
                qa, ka = QA[hp], KA[hp]
                st = []

                def s_load():
                    S.dma(sp, qa[0:64, :], QT[64 * h:64 * h + 64, :], reads=b_qt, writes=[b_QA[hp]])
                    S.dma(sp, ka[0:64, :], KT[64 * h:64 * h + 64, :], reads=b_kt, writes=[b_KA[hp]])
                    S.dma(sp, TBs[hp][:], tb_d[h], writes=[b_TBs[hp]])
                    S.op(dve, lambda: V.scalar_tensor_tensor(BT[hp][:, 0, :, :], TBs[hp][:, 0, :, :], B31[:, h:h + 1], CMs[:, :, :],
                                                             ALU.subtract, ALU.add),
                         reads=[b_TBs[hp], b_B31, b_CM], writes=[b_BT[hp]])
                    S.op(dve, lambda: V.tensor_scalar(BT[hp][:, 1, :, :], TBs[hp][:, 1, :, :], B31[:, h:h + 1], None, ALU.subtract),
                         reads=[b_TBs[hp], b_B31], writes=[b_BT[hp]])
                    S.op(dve, lambda: V.tensor_reduce(KM32[:, :], ka[0:64, :].rearrange("p (n k) -> p n k", k=256), AX.X, ALU.add),
                         reads=[b_KA[hp]], writes=[b_KM32])
                    S.op(dve, lambda: V.tensor_scalar(KMb[hp][:, :], KM32[:, :], 1.0 / 256.0, None, ALU.mult),
                         reads=[b_KM32], writes=[b_KMb[hp]])
                st.append(s_load)

                def s_gate_mm():
                    bi = XB_[0]
                    bk, bb = banks[bi], b_bank[bi]
                    for qt in range(32):
                        S.op(pe, lambda qt=qt: PE.matmul(bk[:, qt * 16:(qt + 1) * 16], qa[0:64, qt * 128:(qt + 1) * 128], KMb[hp][:, :],
                                                         start=True, stop=True),
                             reads=[b_QA[hp], b_KMb[hp]], excl=[bb], inc=(qt == 31))
                    S.op(dve, lambda: V.tensor_tensor(Gs[:, :, :], bk[:, :].rearrange("p (a b) -> p a b", b=16), PMs[:, :, :], ALU.add),
                         reads=[b_PM], excl=[bb], writes=[b_Gs])
                    for qt in range(32):
                        S.op(dve, lambda qt=qt: V.max(T8[:, qt, :], Gs[:, qt, :]), reads=[b_Gs], writes=[b_T8])
                    S.op(dve, lambda: V.tensor_tensor(Ts[:, :, :], Gs[:, :, :], T8[:, :, 2:3].broadcast_to([128, 32, 16]), ALU.is_ge),
                         reads=[b_Gs, b_T8], writes=[b_Ts])
                    S.op(dve, lambda: V.scalar_tensor_tensor(MBP[:, :, 64:80], Ts[:, :, :], 1.0, BPs[:, :, :], ALU.subtract, ALU.mult),
                         reads=[b_Ts, b_BP], writes=[b_MBP])
                st.append(s_gate_mm)

                def s_transpose():
                    for grp in range(4):
                        bi = XB_[1] if grp % 2 == 0 else XB_[0]
                        bk, bb = banks[bi], b_bank[bi]
                        bkb = bk[:, :].bitcast(BF16)
                        for i in range(8):
                            qt = grp * 8 + i
                            S.op(pe, lambda qt=qt, i=i: PE.transpose(bkb[0:80, i * 128:(i + 1) * 128], MBP[:, qt, 0:80], ident[:, :]),
                                 reads=[b_MBP, b_ident], excl=[bb], inc=(i == 7))
                        S.op(dve, lambda grp=grp: V.tensor_copy(qa[64:80, grp * 1024:(grp + 1) * 1024], bkb[64:80, :]),
                             excl=[bb], writes=[b_QAm[hp]])
                st.append(s_transpose)
                return st

            def main_steps(h):
                hp = h % 2
                qa, ka = QA[hp], KA[hp]
                steps = [(j, n) for j in range(16) for n in range(j + 1)]
                return steps

            sctr = [0]
            octr = [0]

            def emit_S(h, j, n, slot):
                hp = h % 2
                qa, ka = QA[hp], KA[hp]
                bi = SB[slot % 4]
                bk, bb = banks[bi], b_bank[bi]
                near = n >= j - 1
                for kt in range(2):
                    k0 = n * 256 + kt * 128
                    S.op(pe, lambda kt=kt, k0=k0: PE.matmul(bk[:, kt * 256:(kt + 1) * 256], ka[0:80, k0:k0 + 128], qa[0:80, j * 256:(j + 1) * 256],
                                                            start=True, stop=not near),
                         reads=[b_KA[hp], b_KAo[hp], b_QA[hp], b_QAm[hp]], excl=[bb], inc=(kt == 1 and not near))
                    if near:
                        kind = 0 if n == j else 1
                        S.op(pe, lambda kt=kt, kind=kind: PE.matmul(bk[:, kt * 256:(kt + 1) * 256], ident[:, :], BT[hp][:, kind, kt, :],
                                                                    start=False, stop=True),
                             reads=[b_ident, b_BT[hp]], excl=[bb], inc=(kt == 1))
                pt, bpt = PT[slot % NPT], b_PT[slot % NPT]
                S.op(act, lambda: A.activation(pt[:, :], bk[:, :], AF.Exp), excl=[bb], writes=[bpt])

            def emit_PV(h, j, n, slot, oslot):
                pt, bpt = PT[slot % NPT], b_PT[slot % NPT]
                bi = OB[oslot % 2]
                bk, bb = banks[bi], b_bank[bi]
                for kt in range(2):
                    s_idx = n * 2 + kt
                    S.op(pe, lambda kt=kt, s_idx=s_idx: PE.matmul(bk[0:65, 0:256], VA[:, s_idx, h, 0:65], pt[:, kt * 256:(kt + 1) * 256],
                                                                  start=(n == 0 and kt == 0), stop=(n == j and kt == 1)),
                         reads=[bpt, b_VA[s_idx // 8], b_VA1], excl=[bb], inc=(kt == 1))

            def emit_norm1(h, j, oslot):
                hp = h % 2
                bi = OB[oslot % 2]
                bk, bb = banks[bi], b_bank[bi]
                os_, bos = OS[oslot % NOS], b_OS[oslot % NOS]
                rrt, brr = RR[oslot % NOS], b_RR[oslot % NOS]
                S.op(dve, lambda: V.tensor_copy(os_[:, :], bk[0:64, 0:256]), excl=[bb], writes=[bos])
                S.op(dve, lambda: V.reciprocal(rrt[64:65, :], bk[64:65, 0:256]), excl=[bb], writes=[brr])

            def emit_norm2(h, j, oslot):
                hp = h % 2
                os_, bos = OS[oslot % NOS], b_OS[oslot % NOS]
                rrt, brr = RR[oslot % NOS], b_RR[oslot % NOS]
                bi = XB_[oslot % 2]
                bk, bb = banks[bi], b_bank[bi]
                S.op(pe, lambda: PE.matmul(bk[0:64, 0:256], ones32[64:65, 0:64], rrt[64:65, :], start=True, stop=True),
                     reads=[b_ones32, brr], excl=[bb])
                S.op(dve, lambda: V.tensor_tensor(OT[hp][:, j * 256:(j + 1) * 256], os_[:, :], bk[0:64, 0:256], ALU.mult),
                     reads=[bos], excl=[bb], writes=[b_OT[hp]])

            st0 = prologue_stages(0)
            st0[0]()
            for i in range(4):
                S.dma(sp, VA[:, i * 8:(i + 1) * 8, :, :].rearrange("p s h d -> p s (h d)"),
                      VS.rearrange("(s p) f -> p s f", p=128)[:, i * 8:(i + 1) * 8, :],
                      reads=[b_vs[2 * i], b_vs[2 * i + 1]], writes=[b_VA[i]])
            for f in st0[1:]:
                f()
            for h in range(8):
                hp = h % 2
                steps = [(j, n) for j in range(16) for n in range(j + 1)]
                nxt = prologue_stages(h + 1) if h + 1 < 8 else []
                stage_at = {8: 0, 40: 1, 80: 2}
                pending_norm = []
                base = sctr[0]
                emit_S(h, steps[0][0], steps[0][1], base)
                emit_S(h, steps[1][0], steps[1][1], base + 1)
                for si, (j, n) in enumerate(steps):
                    if si + 2 < len(steps):
                        emit_S(h, steps[si + 2][0], steps[si + 2][1], base + si + 2)
                    emit_PV(h, j, n, base + si, octr[0])
                    while pending_norm and pending_norm[0][0] <= si:
                        _, jj, osl = pending_norm.pop(0)
                        emit_norm2(h, jj, osl)
                    if n == j:
                        emit_norm1(h, j, octr[0])
                        pending_norm.append((si + 5, j, octr[0]))
                        octr[0] += 1
                    if si in stage_at and nxt:
                        nxt[stage_at[si]]()
                    job_tick()
                while pending_norm:
                    _, jj, osl = pending_norm.pop(0)
                    emit_norm2(h, jj, osl)
                sctr[0] = base + len(steps)
                S.dma(sp, CAT[512 + 64 * h:512 + 64 * h + 64, :], OT[hp][:, :], reads=[b_OT[hp]], writes=[b_cath[h]])
            assert jstate["cast"] == len(jobs), (jstate, len(jobs))
        S.barrier()

    def phase_C(l, Wo, b_Wo):
        with contextlib.ExitStack() as es:
            CT = [sbt(es, f"CT{i}", [128, 8, T], BF16) for i in range(2)]; b_CT = [Buf(), Buf()]
            XT = [sbt(es, f"XTc{i}", [128, 8, T], F32) for i in range(2)]
            b_XT = [[Buf() for _ in range(8)] for _ in range(2)]
            Y = sbt(es, "Yc", [128, 8, T], F32); b_Y = [Buf() for _ in range(8)]
            SQ = sbt(es, "SQc", [128, 8, T], BF16); b_SQ = [Buf() for _ in range(8)]
            R = sbt(es, "Rc", [128, T], F32); b_R = Buf()
            src = xT if l == 0 else xs

            def load_ct(it):
                S.dma(sp, CT[it % 2][:, :, :], fm(CAT)[:, :, cols(it)], reads=[b_catp[it]] + b_cath, writes=[b_CT[it % 2]])

            def load_x(it):
                rd = [] if l == 0 else b_xs[it]
                S.dma(sp, XT[it % 2][:, :, :], fm(src)[:, :, cols(it)], reads=rd, writes=b_XT[it % 2])

            load_ct(0)
            load_x(0)
            for it in range(NT):
                if it + 1 < NT:
                    load_ct(it + 1)
                    load_x(it + 1)
                ct, bct = CT[it % 2], b_CT[it % 2]
                xt, bx = XT[it % 2], b_XT[it % 2]
                for oc in range(8):
                    oc_s = slice(oc * 128, (oc + 1) * 128)
                    bk, bb = next_bank()
                    for kc in range(8):
                        S.op(pe, lambda kc=kc: PE.matmul(bk[:, :], Wo[:, kc, oc_s], ct[:, kc, :], start=(kc == 0), stop=(kc == 7)),
                             reads=[b_Wo[kc], bct], excl=[bb], inc=(kc == 7))
                    if oc % 2 == 0:
                        S.op(dve, lambda: V.tensor_copy(Y[:, oc, :], bk[:, :]), excl=[bb], writes=[b_Y[oc]])
                    else:
                        S.op(act, lambda: A.copy(Y[:, oc, :], bk[:, :]), excl=[bb], writes=[b_Y[oc]])
                post_norm_residual(Y, b_Y, SQ, b_SQ, R, b_R, xt, bx, l, 1)
                S.dma(sp, fm(xs)[:, :, cols(it)], xt[:, :, :], reads=bx, writes=b_xs[it])
        S.barrier()

    def alloc_Wug(es, l):
        Wug = sbt(es, "Wug", [128, 8, DFF], BF16); b_Wug = [[Buf(), Buf()] for _ in range(8)]
        src = w_up[l].rearrange("(kc p) n -> p kc n", p=128)
        HC = DFF // 2
        for hf in range(2):
            for kc in range(8):
                S.dma(pool, Wug[:, kc, hf * HC:(hf + 1) * HC], src[:, kc, hf * HC:(hf + 1) * HC], writes=[b_Wug[kc][hf]])
        return Wug, b_Wug

    def alloc_Wo(es, l):
        Wo = sbt(es, "Wo", [128, 8, DM], BF16); b_Wo = [Buf() for _ in range(8)]
        src = w_out[l].rearrange("(kc p) n -> p kc n", p=128)
        jobs = [(Wo[:, kc, :], src[:, kc, :], b_Wo[kc]) for kc in range(8)]
        return Wo, b_Wo, jobs

    def alloc_Wuv(es, l):
        Wuv = sbt(es, "Wuv", [128, 8, DFF], BF16); b_Wuv = [Buf() for _ in range(8)]
        src = w_up[l].rearrange("(kc p) n -> p kc n", p=128)
        HC = DFF // 2
        jobs = []
        for kc in range(8):
            for hf in range(2):
                jobs.append((Wuv[:, kc, hf * HC:(hf + 1) * HC], src[:, kc, DFF + hf * HC:DFF + (hf + 1) * HC], b_Wuv[kc]))
        return Wuv, b_Wuv, jobs

    def alloc_Wd(es, l):
        Wd = sbt(es, "Wd", [128, NFC, DM], BF16); b_Wd = [Buf() for _ in range(NFC)]
        load_weight(Wd, w_down[l].rearrange("(kc p) n -> p kc n", p=128), NFC, b_Wd)
        return Wd, b_Wd

    def phase_D(l, Wug, b_Wug, Wuv, b_Wuv, Wd, b_Wd):
        with contextlib.ExitStack() as es:
            Y = sbt(es, "Yd", [128, 8, T], F32); b_Y = [Buf() for _ in range(8)]
            H = sbt(es, "Hd", [128, 8, T], BF16); b_H = [Buf() for _ in range(8)]
            AT = sbt(es, "AT", [128, NFC, T], BF16); b_AT = [Buf() for _ in range(NFC)]
            SQY0 = NFC - 8
            R1 = sbt(es, "R1d", [128, T], F32); b_R1 = Buf()
            R2 = sbt(es, "R2d", [128, T], F32); b_R2 = Buf()
            NXS = 4
            XS = [sbt(es, f"XS{i}", [128, T], F32) for i in range(NXS)]; b_XS = [Buf() for _ in range(NXS)]
            xs_ctr = [0]
            Gst = [sbt(es, f"Gst{i}", [128, T + 2], F32) for i in range(2)]; b_Gst = [Buf(), Buf()]
            Cv = [sbt(es, f"Cv{i}", [128, T], F32) for i in range(2)]; b_Cv = [Buf(), Buf()]
            hist = sbt(es, "hist", [128, NFC, 2], F32); b_hist = Buf()
            S.op(pool, lambda: P.memset(hist[:, :, :], 0.0), writes=[b_hist])

            def xs_next():
                i = xs_ctr[0] % NXS
                xs_ctr[0] += 1
                return XS[i], b_XS[i]

            def xchunk(it, c):
                return fm(xs)[:, c, cols(it)]

            def ss_rstd(sq_ap, b_sq, R, b_R):
                bk, bb = next_bank()
                for c in range(8):
                    S.op(pe, lambda c=c: PE.matmul(bk[:, :], ones_b[:, :], sq_ap(c), start=(c == 0), stop=(c == 7)),
                         reads=[b_sq[c], b_ones], excl=[bb], inc=(c == 7))
                S.op(act, lambda: A.activation(R[:, :], bk[:, :], AF.Sqrt, bias=eps_t[:, 0:1], scale=1.0 / DM),
                     reads=[b_eps], excl=[bb], writes=[b_R])
                S.op(dve, lambda: V.reciprocal(R[:, :], R[:, :]), reads=[b_R], writes=[b_R])

            p1slots = {}

            def pre_loads1(it, cs):
                for c in cs:
                    xt_, bx_ = xs_next()
                    S.dma(sp, xt_[:, :], xchunk(it, c), reads=[b_xs[it][c]], writes=[bx_])
                    p1slots[(it, c)] = (xt_, bx_)

            def pre_sq1(it, cs):
                for c in cs:
                    xt_, bx_ = p1slots.pop((it, c))
                    S.op(act, lambda c=c, xt_=xt_: A.activation(H[:, c, :], xt_[:, :], AF.Square), reads=[bx_], writes=[b_H[c]])

            def pre_pass1(it):
                pre_loads1(it, range(0, 4))
                pre_sq1(it, range(0, 4))
                pre_loads1(it, range(4, 8))
                pre_sq1(it, range(4, 8))

            def pre_ss(it):
                ss_rstd(lambda c: H[:, c, :], b_H, R1, b_R1)

            def pre_pass2(it):
                for c in range(8):
                    xt_, bx_ = xs_next()
                    S.dma(sp, xt_[:, :], xchunk(it, c), reads=[b_xs[it][c]], writes=[bx_])
                    S.op(dve, lambda c=c, xt_=xt_: V.scalar_tensor_tensor(H[:, c, :], xt_[:, :], gv[:, l, 2, c:c + 1], R1[:, :], ALU.mult, ALU.mult),
                         reads=[bx_, b_gv, b_R1], writes=[b_H[c]])

            def post_sq(it):
                for c in range(8):
                    S.op(act, lambda c=c: A.activation(AT[:, SQY0 + c, :], Y[:, c, :], AF.Square),
                         reads=[b_Y[c]], writes=[b_AT[SQY0 + c]])

            def post_ss(it):
                ss_rstd(lambda c: AT[:, SQY0 + c, :], b_AT[SQY0:], R2, b_R2)

            PST = [AT[:, 16 + 2 * i:18 + 2 * i, :].rearrange("p c t -> p (c t)").bitcast(F32) for i in range(3)]
            b_PST = [[b_AT[16 + 2 * i], b_AT[17 + 2 * i]] for i in range(3)]

            def post_load(it, c):
                S.dma(sp, PST[c % 3], xchunk(it, c), reads=[b_xs[it][c]], writes=b_PST[c % 3])

            def post_piece(it, c):
                pst, bpst = PST[c % 3], b_PST[c % 3]
                S.op(dve, lambda: V.scalar_tensor_tensor(Y[:, c, :], Y[:, c, :], gv[:, l, 3, c:c + 1], R2[:, :], ALU.mult, ALU.mult),
                     reads=[b_Y[c], b_gv, b_R2], writes=[b_Y[c]])
                S.op(dve, lambda: V.tensor_tensor(pst, pst, Y[:, c, :], ALU.add),
                     reads=bpst + [b_Y[c]], writes=bpst)
                S.dma(sp, xchunk(it, c), pst, reads=bpst, writes=[b_xs[it][c]])
                if c + 3 < 8:
                    post_load(it, c + 3)

            def post_res(it):
                for c in range(3):
                    post_load(it, c)
                for c in range(8):
                    post_piece(it, c)

            pre_pass1(0)
            pre_ss(0)
            pre_pass2(0)
            for it in range(NT):
                for c in range(NFC):
                    gs, bgs = Gst[c % 2], b_Gst[c % 2]
                    cv, bcv = Cv[c % 2], b_Cv[c % 2]
                    g_s = slice(c * 128, (c + 1) * 128)
                    bg, bbg = next_bank()
                    for kc in range(8):
                        S.op(pe, lambda kc=kc: PE.matmul(bg[:, :], Wug[:, kc, g_s], H[:, kc, :], start=(kc == 0), stop=(kc == 7)),
                             reads=[b_Wug[kc][c // 11], b_H[kc]], excl=[bbg], inc=(kc == 7))
                    bv, bbv = next_bank()
                    for kc in range(8):
                        S.op(pe, lambda kc=kc: PE.matmul(bv[:, :], Wuv[:, kc, g_s], H[:, kc, :], start=(kc == 0), stop=(kc == 7)),
                             reads=[b_Wuv[kc], b_H[kc]], excl=[bbv], inc=(kc == 7))
                    S.op(pool, lambda: P.tensor_copy(gs[:, 0:2], hist[:, c, :]), reads=[b_hist], writes=[bgs])
                    S.op(act, lambda: A.copy(gs[:, 2:T + 2], bg[:, :]), excl=[bbg], writes=[bgs])
                    S.op(pool, lambda: P.tensor_copy(hist[:, c, :], gs[:, T:T + 2]), reads=[bgs], writes=[b_hist])
                    S.op(act, lambda: A.activation(cv[:, :], bg[:, :], AF.Identity, bias=cb[:, l, c:c + 1], scale=cw[:, l, c, 2:3]),
                         reads=[b_cw, b_cb], excl=[bbg], writes=[bcv])
                    S.op(dve, lambda: V.scalar_tensor_tensor(cv[:, :], gs[:, 1:T + 1], cw[:, l, c, 1:2], cv[:, :], ALU.mult, ALU.add),
                         reads=[bgs, b_cw, bcv], writes=[bcv])
                    S.op(dve, lambda: V.scalar_tensor_tensor(cv[:, :], gs[:, 0:T], cw[:, l, c, 0:1], cv[:, :], ALU.mult, ALU.add),
                         reads=[bgs, b_cw, bcv], writes=[bcv])
                    S.op(act, lambda: A.activation(cv[:, :], cv[:, :], AF.Gelu_apprx_tanh), reads=[bcv], writes=[bcv])
                    S.op(dve, lambda: V.tensor_tensor(AT[:, c, :], cv[:, :], bv[:, :], ALU.mult),
                         reads=[bcv], excl=[bbv], writes=[b_AT[c]])
                    if it > 0:
                        if c == 0:
                            post_sq(it - 1)
                        elif c == 2:
                            post_ss(it - 1)
                        elif c == 4:
                            for cc in range(3):
                                post_load(it - 1, cc)
                        elif 5 <= c < 13:
                            post_piece(it - 1, c - 5)
                    if c == 16 and it + 1 < NT:
                        pre_loads1(it + 1, range(0, 4))
                if it + 1 < NT:
                    pre_sq1(it + 1, range(0, 4))
                    pre_loads1(it + 1, range(4, 8))
                    pre_sq1(it + 1, range(4, 8))
                for oc in range(8):
                    oc_s = slice(oc * 128, (oc + 1) * 128)
                    bk, bb = next_bank()
                    for kc in range(NFC):
                        S.op(pe, lambda kc=kc: PE.matmul(bk[:, :], Wd[:, kc, oc_s], AT[:, kc, :], start=(kc == 0), stop=(kc == NFC - 1)),
                             reads=[b_Wd[kc], b_AT[kc]], excl=[bb], inc=(kc == NFC - 1))
                    if oc % 2 == 0:
                        S.op(act, lambda: A.copy(Y[:, oc, :], bk[:, :]), excl=[bb], writes=[b_Y[oc]])
                    else:
                        S.op(dve, lambda: V.tensor_copy(Y[:, oc, :], bk[:, :]), excl=[bb], writes=[b_Y[oc]])
                    if it + 1 < NT:
                        if oc == 1:
                            pre_ss(it + 1)
                        elif oc == 2:
                            pre_pass2(it + 1)
            post_sq(NT - 1)
            post_ss(NT - 1)
            post_res(NT - 1)
        S.barrier()

    def phase_E(l):
        with contextlib.ExitStack() as es:
            Wg = sbt(es, "Wg", [128, 8, DM], BF16); b_Wg = [Buf() for _ in range(8)]
            Wp2 = sbt(es, "Wp2", [128, 2, DM], BF16); b_Wp2 = [Buf() for _ in range(2)]
            load_weight(Wg, w_gate[l].rearrange("(kc p) n -> p kc n", p=128), 8, b_Wg)
            load_weight(Wp2, w_ple[l].rearrange("(kc p) n -> p kc n", p=128), 2, b_Wp2)
            XBs = [sbt(es, f"XBe{i}", [128, 8, T], BF16) for i in range(2)]
            b_XBs = [[Buf() for _ in range(8)] for _ in range(2)]
            PB = [sbt(es, f"PBe{i}", [128, 2, T], BF16) for i in range(2)]; b_PB = [Buf(), Buf()]
            SG = [sbt(es, f"SG{i}", [128, T], F32) for i in range(2)]; b_SG = [Buf(), Buf()]
            TM = [sbt(es, f"TM{i}", [128, T], F32) for i in range(2)]; b_TM = [Buf(), Buf()]
            XT = [sbt(es, f"XTe{i}", [128, 8, T], F32) for i in range(3)]
            b_XT = [[Buf() for _ in range(8)] for _ in range(3)]

            def load_x(it):
                S.dma(sp, XT[it % 3][:, :, :], fm(xs)[:, :, cols(it)], reads=b_xs[it], writes=b_XT[it % 3])
            load_x(0)
            ple_load_p(l, 0, PB[0], b_PB[0])
            for it in range(NT):
                if it + 1 < NT:
                    load_x(it + 1)
                    ple_load_p(l, it + 1, PB[(it + 1) % 2], b_PB[(it + 1) % 2])
                xt, bx = XT[it % 3], b_XT[it % 3]
                ple(l, it, xt, bx, XBs[it % 2], b_XBs[it % 2], PB[it % 2], b_PB[it % 2], Wg, b_Wg, Wp2, b_Wp2, SG, b_SG, TM, b_TM)
                S.dma(sp, fm(outT)[:, :, cols(it)], xt[:, :, :], reads=bx, writes=[b_out[it]])
        S.barrier()

    def run_all():
        for l in layers:
            phase_A(l)
            if stop_after == ("A", l):
                return
            with contextlib.ExitStack() as esW:
                Wuv, b_Wuv, jobs = alloc_Wuv(esW, l)
                with contextlib.ExitStack() as esO:
                    Wo, b_Wo, jobs_o = alloc_Wo(esO, l)
                    phase_B(l, jobs_o + jobs)
                    if stop_after == ("B", l):
                        return
                    phase_C(l, Wo, b_Wo)
                    if stop_after == ("C", l):
                        return
                Wug, b_Wug = alloc_Wug(esW, l)
                Wd, b_Wd = alloc_Wd(esW, l)
                phase_D(l, Wug, b_Wug, Wuv, b_Wuv, Wd, b_Wd)
                if stop_after == ("D", l):
                    return
        phase_E(layers[-1])
    run_all()
    S.finish()
    es0.close()
    S.close()
    return S
import math
import ml_dtypes
from concourse.bass_utils import run_bass_kernel_spmd

_BF = ml_dtypes.bfloat16


def _rel_bucket_np(n):
    n = np.maximum(n, 0)
    max_exact = 16
    nf = np.maximum(n, 1).astype(np.float32)
    large = max_exact + (np.log(nf / np.float32(max_exact)) / np.float32(math.log(128 / max_exact))
                         * np.float32(32 - max_exact)).astype(np.int32)
    large = np.minimum(large, 31)
    return np.where(n < max_exact, n, large)


def _static_consts():
    c = {}
    c["ident"] = np.eye(128, dtype=np.float32).astype(_BF)
    oh = np.zeros((16, SEQ), np.float32)
    for n in range(16):
        oh[n, n * 256:(n + 1) * 256] = 1.0
    c["onehot"] = oh.astype(_BF)
    pm = np.zeros((32, 16), np.float32)
    bp = np.zeros((32, 16), np.float32)
    for qt in range(32):
        j = qt // 2
        pm[qt, j:] = -BIG
        bp[qt, :j] = BIG
    c["pmall"] = np.ascontiguousarray(np.broadcast_to(pm, (128, 32, 16)))
    c["bigpast"] = np.ascontiguousarray(np.broadcast_to(bp, (128, 32, 16)))
    k = np.arange(256)[:, None]
    q = np.arange(256)[None, :]
    cm = np.where(q - k >= 0, 0.0, -BIG).astype(np.float32)
    c["cm"] = np.ascontiguousarray(cm.reshape(2, 128, 256).transpose(1, 0, 2))
    rc = np.zeros((4, 16), np.float32)
    for g, w in enumerate((2, 4, 8, 16)):
        rc[g] = 1.0 / np.minimum(np.arange(16) + 1, w)
    c["rc"] = np.ascontiguousarray(np.broadcast_to(rc, (128, 4, 16)))
    return c


def _prep_shared(inp):
    f = lambda a: np.ascontiguousarray(np.asarray(a, dtype=np.float32))
    d = {}
    for k_src, k_dst in (("w_in", "w_in"), ("w_pool", "w_pool"), ("w_out", "w_out"), ("w_up", "w_up"),
                         ("w_down", "w_down"), ("w_ple", "w_ple"), ("w_ple_gate", "w_gate")):
        d[k_dst] = f(inp[k_src])
    g = np.stack([f(inp["g_mix_pre"]), f(inp["g_mix_post"]), f(inp["g_ffn_pre"]), f(inp["g_ffn_post"])], axis=1)
    d["gvec"] = np.ascontiguousarray(g.reshape(2, 4, 8, 128).transpose(3, 0, 1, 2))
    d["pscale"] = np.ascontiguousarray(f(inp["pool_scale"]).reshape(2, 4, 128).transpose(2, 0, 1))
    d["convw"] = np.ascontiguousarray(f(inp["conv_w"]).reshape(2, 3, NFC, 128).transpose(3, 0, 2, 1))
    d["convb"] = np.ascontiguousarray(f(inp["conv_b"]).reshape(2, NFC, 128).transpose(2, 0, 1))
    rb = f(inp["rel_bias"])
    k = np.arange(256)[:, None]
    q = np.arange(256)[None, :]
    idx_own = _rel_bucket_np(q - k)
    idx_prev = _rel_bucket_np(q + 256 - k)
    tb = np.stack([rb[idx_own], rb[idx_prev]], axis=0)
    tb = tb.reshape(2, 2, 128, 256, 8).transpose(4, 2, 0, 1, 3)
    d["tb"] = np.ascontiguousarray(tb)
    d["b31"] = np.ascontiguousarray(np.broadcast_to(rb[31][None, :], (128, 8)))
    d.update(_static_consts())
    return d


_CACHE = {}


def _get_nc(debug=False, stop_after=None, layers=(0, 1)):
    key = (debug, stop_after, layers)
    if key not in _CACHE:
        nc = bass.Bass("TRN2", target_bir_lowering=False)
        build_program(nc, debug=debug, stop_after=stop_after, layers=layers)
        _CACHE[key] = nc
    return _CACHE[key]


def kernel(**inputs):
    x = np.asarray(inputs["x"], dtype=np.float32)
    p = np.asarray(inputs["p"], dtype=np.float32)
    shared = _prep_shared(inputs)
    n = x.shape[0]
    in_maps = []
    for b in range(n):
        m = dict(shared)
        m["xT"] = np.ascontiguousarray(x[b].T)
        m["pT"] = np.ascontiguousarray(p[:, b].transpose(0, 2, 1))
        in_maps.append(m)
    nc = _get_nc()
    res = run_bass_kernel_spmd(nc, in_maps, core_ids=list(range(n)))
    out = np.stack([np.asarray(r["outT"], dtype=np.float32).T for r in res.results], axis=0)
    return np.ascontiguousarray(out)
```
